# Optimizing a Trainium2 kernel written in Bass

```python
import jax, jax.numpy as jnp
from jax import lax
import numpy as np

D_MODEL = 2048
BATCH = 4
SEQ = 8192
DEPTH = 1

D_MIX = D_MODEL
RWKV_WIDTH = D_MIX // 2
LRU_WIDTH = D_MIX - RWKV_WIDTH
HEAD_DIM = 64
RWKV_HEADS = RWKV_WIDTH // HEAD_DIM
LRU_BLOCKS = 16
LRU_BLOCK_DIM = LRU_WIDTH // LRU_BLOCKS
DECAY_LORA = max(32, int(round(1.8 * RWKV_WIDTH ** 0.5 / 32)) * 32)
AAA_LORA = max(32, int(round(1.8 * RWKV_WIDTH ** 0.5 / 32)) * 32)
GATE_LORA = max(32, int(round(0.6 * RWKV_WIDTH ** 0.8 / 32)) * 32)
RWKV_IN = 3 * RWKV_WIDTH + DECAY_LORA + AAA_LORA + GATE_LORA
LRU_IN = 2 * LRU_WIDTH
D_IN = RWKV_IN + LRU_IN
D_FF = 11 * D_MODEL // 4
CONV_WIDTH = 4
LRU_C = 8.0
NORM_EPS = 1e-6
GN_EPS = 64e-5

kernel_name = "hybrid_rwkv7_rglru_macaron"


def _split(t, sizes):
    out, start = [], 0
    for s in sizes:
        out.append(t[..., start:start + s])
        start += s
    return out


def rms_norm(x, g):
    xf = x.astype(jnp.float32)
    y = xf * lax.rsqrt(jnp.mean(xf * xf, axis=-1, keepdims=True) + NORM_EPS)
    return (y * g.astype(jnp.float32)).astype(x.dtype)


def swiglu(x, w_gate, w_up, w_down):
    return (jax.nn.silu(x @ w_gate) * (x @ w_up)) @ w_down


def shift_prev(t):
    return jnp.pad(t, ((0, 0), (1, 0), (0, 0)))[:, :-1]


def rwkv7_time_mix(p, mu, w0, w2, a0, a2, g2, k_k, k_a, r_k, ln_w, ln_b):
    B, S, _ = p.shape
    f32 = jnp.float32
    p = p + (shift_prev(p) - p) * mu
    r, xw, k, v, xa, xg = _split(p, (RWKV_WIDTH, DECAY_LORA, RWKV_WIDTH, RWKV_WIDTH, AAA_LORA, GATE_LORA))
    w = -jax.nn.softplus(-(w0 + jnp.tanh(xw) @ w2)) - 0.5
    a = jax.nn.sigmoid(a0 + xa @ a2)
    g = jax.nn.sigmoid(xg) @ g2
    heads = lambda t: t.astype(f32).reshape(B, S, RWKV_HEADS, HEAD_DIM)
    kk = heads(k * k_k)
    kk = kk / jnp.maximum(jnp.sqrt(jnp.sum(kk * kk, axis=-1, keepdims=True)), 1e-12)
    k = heads(k * (1.0 + (a - 1.0) * k_a))
    r, v, a = heads(r), heads(v), heads(a)
    decay = jnp.exp(-jnp.exp(heads(w)))
    tm = lambda t: jnp.swapaxes(t, 0, 1)

    def step(state, inp):
        r_t, d_t, k_t, v_t, kk_t, a_t = inp
        sa = jnp.einsum('bhij,bhj->bhi', state, -kk_t)
        state = (state * d_t[:, :, None, :]
                 + sa[..., None] * (kk_t * a_t)[:, :, None, :]
                 + v_t[..., None] * k_t[:, :, None, :])
        return state, jnp.einsum('bhij,bhj->bhi', state, r_t)

    state0 = jnp.zeros((B, RWKV_HEADS, HEAD_DIM, HEAD_DIM), f32)
    _, y = lax.scan(step, state0, (tm(r), tm(decay), tm(k), tm(v), tm(kk), tm(a)))
    y = tm(y)
    mean = jnp.mean(y, axis=-1, keepdims=True)
    var = jnp.mean(jnp.square(y - mean), axis=-1, keepdims=True)
    y = ((y - mean) * lax.rsqrt(var + GN_EPS)).reshape(B, S, RWKV_WIDTH)
    y = y * ln_w.astype(f32) + ln_b.astype(f32)
    bonus = jnp.sum(r * k * r_k.astype(f32), axis=-1, keepdims=True) * v
    y = y + bonus.reshape(B, S, RWKV_WIDTH)
    return (y * g.astype(f32)).astype(p.dtype)


def rglru_mix(p, conv_w, conv_b, wa, ba, wx, bx, lam, norm_g):
    B, S, _ = p.shape
    f32 = jnp.float32
    xb, gate = _split(p, (LRU_WIDTH, LRU_WIDTH))
    xc = lax.conv_general_dilated(
        xb, conv_w[:, None, :], window_strides=(1,), padding=[(CONV_WIDTH - 1, 0)],
        dimension_numbers=('NWC', 'WIO', 'NWC'), feature_group_count=LRU_WIDTH) + conv_b
    blocks = xc.reshape(B, S, LRU_BLOCKS, LRU_BLOCK_DIM)
    r = jax.nn.sigmoid(jnp.einsum('bshi,hij->bshj', blocks, wa).reshape(B, S, LRU_WIDTH) + ba)
    i = jax.nn.sigmoid(jnp.einsum('bshi,hij->bshj', blocks, wx).reshape(B, S, LRU_WIDTH) + bx)
    log_a = -LRU_C * r.astype(f32) * jax.nn.softplus(-lam.astype(f32))
    a = jnp.exp(log_a)
    u = jnp.sqrt(-jnp.expm1(2.0 * log_a)) * (i * xc).astype(f32)

    def step(h, au):
        a_t, u_t = au
        h = a_t * h + u_t
        return h, h

    _, h = lax.scan(step, jnp.zeros((B, LRU_WIDTH), f32),
                    (jnp.swapaxes(a, 0, 1), jnp.swapaxes(u, 0, 1)))
    h = jnp.swapaxes(h, 0, 1).astype(p.dtype)
    y = h * jax.nn.gelu(gate)
    return rms_norm(y, norm_g)


def setup_inputs(seed: int = 0) -> dict:
    key = jax.random.key(seed)
    ks = iter(jax.random.split(key, 40))
    f32 = jnp.float32
    L = DEPTH
    nrm = lambda shape, scale: jax.random.normal(next(ks), shape, f32) * scale
    gain = lambda shape: 1.0 + nrm(shape, 0.02)
    uni = lambda shape, lo, hi: jax.random.uniform(next(ks), shape, f32, lo, hi)
    x = nrm((BATCH, SEQ, D_MODEL), 1.0)
    d = {"x": x}
    d["ffn1_norm"] = gain((L, D_MODEL))
    d["ffn1_w_gate"] = nrm((L, D_MODEL, D_FF), D_MODEL ** -0.5)
    d["ffn1_w_up"] = nrm((L, D_MODEL, D_FF), D_MODEL ** -0.5)
    d["ffn1_w_down"] = nrm((L, D_FF, D_MODEL), D_FF ** -0.5)
    d["mix_norm"] = gain((L, D_MODEL))
    d["w_in"] = nrm((L, D_MODEL, D_IN), D_MODEL ** -0.5)
    d["rwkv_mu"] = uni((L, RWKV_IN), 0.0, 1.0)
    d["rwkv_w0"] = uni((L, RWKV_WIDTH), -6.0, 1.0)
    d["rwkv_w2"] = nrm((L, DECAY_LORA, RWKV_WIDTH), DECAY_LORA ** -0.5)
    d["rwkv_a0"] = nrm((L, RWKV_WIDTH), 0.1)
    d["rwkv_a2"] = nrm((L, AAA_LORA, RWKV_WIDTH), AAA_LORA ** -0.5)
    d["rwkv_g2"] = nrm((L, GATE_LORA, RWKV_WIDTH), GATE_LORA ** -0.5)
    d["rwkv_k_k"] = 0.85 + nrm((L, RWKV_WIDTH), 0.02)
    d["rwkv_k_a"] = gain((L, RWKV_WIDTH))
    d["rwkv_r_k"] = nrm((L, RWKV_HEADS, HEAD_DIM), 0.1)
    d["rwkv_ln_w"] = gain((L, RWKV_WIDTH))
    d["rwkv_ln_b"] = nrm((L, RWKV_WIDTH), 0.01)
    d["lru_conv_w"] = nrm((L, CONV_WIDTH, LRU_WIDTH), 0.5)
    d["lru_conv_b"] = nrm((L, LRU_WIDTH), 0.01)
    d["lru_wa"] = nrm((L, LRU_BLOCKS, LRU_BLOCK_DIM, LRU_BLOCK_DIM), LRU_BLOCK_DIM ** -0.5)
    d["lru_ba"] = nrm((L, LRU_WIDTH), 0.01)
    d["lru_wx"] = nrm((L, LRU_BLOCKS, LRU_BLOCK_DIM, LRU_BLOCK_DIM), LRU_BLOCK_DIM ** -0.5)
    d["lru_bx"] = nrm((L, LRU_WIDTH), 0.01)
    a_pow = uni((L, LRU_WIDTH), 0.9, 0.999)
    a_base = a_pow ** (1.0 / LRU_C)
    d["lru_lam"] = jnp.log(a_base) - jnp.log1p(-a_base)
    d["lru_norm"] = gain((L, LRU_WIDTH))
    d["w_out"] = nrm((L, D_MIX, D_MODEL), D_MIX ** -0.5)
    d["ffn2_norm"] = gain((L, D_MODEL))
    d["ffn2_w_gate"] = nrm((L, D_MODEL, D_FF), D_MODEL ** -0.5)
    d["ffn2_w_up"] = nrm((L, D_MODEL, D_FF), D_MODEL ** -0.5)
    d["ffn2_w_down"] = nrm((L, D_FF, D_MODEL), D_FF ** -0.5)
    d["final_norm"] = gain((D_MODEL,))
    return d


def reference(x, ffn1_norm, ffn1_w_gate, ffn1_w_up, ffn1_w_down,
              mix_norm, w_in, rwkv_mu, rwkv_w0, rwkv_w2, rwkv_a0, rwkv_a2, rwkv_g2,
              rwkv_k_k, rwkv_k_a, rwkv_r_k, rwkv_ln_w, rwkv_ln_b,
              lru_conv_w, lru_conv_b, lru_wa, lru_ba, lru_wx, lru_bx, lru_lam, lru_norm,
              w_out, ffn2_norm, ffn2_w_gate, ffn2_w_up, ffn2_w_down, final_norm):
    h = x
    for l in range(DEPTH):
        h = h + 0.5 * swiglu(rms_norm(h, ffn1_norm[l]), ffn1_w_gate[l], ffn1_w_up[l], ffn1_w_down[l])
        p = rms_norm(h, mix_norm[l]) @ w_in[l]
        y_rwkv = rwkv7_time_mix(p[..., :RWKV_IN], rwkv_mu[l], rwkv_w0[l], rwkv_w2[l],
                                rwkv_a0[l], rwkv_a2[l], rwkv_g2[l], rwkv_k_k[l], rwkv_k_a[l],
                                rwkv_r_k[l], rwkv_ln_w[l], rwkv_ln_b[l])
        y_lru = rglru_mix(p[..., RWKV_IN:], lru_conv_w[l], lru_conv_b[l], lru_wa[l], lru_ba[l],
                          lru_wx[l], lru_bx[l], lru_lam[l], lru_norm[l])
        h = h + jnp.concatenate([y_rwkv, y_lru], axis=-1) @ w_out[l]
        h = h + 0.5 * swiglu(rms_norm(h, ffn2_norm[l]), ffn2_w_gate[l], ffn2_w_up[l], ffn2_w_down[l])
    return rms_norm(h, final_norm)
```

```python
import contextlib
import numpy as np
import concourse.bass as bass
import concourse.mybir as mybir
from concourse.bass_utils import run_bass_kernel_spmd

F32 = mybir.dt.float32
BF16 = mybir.dt.bfloat16
AF = mybir.ActivationFunctionType
ALU = mybir.AluOpType
AX = mybir.AxisListType

D = 2048
DFF = 5632
SEQ = 8192
T = 256
NTB = 2
DIN = 5408
N_CORES = 4
CD = 0.5 * float(np.exp(-0.5))

G1, GM, G2, MU, W0, A0, KK, KA, RK, CW, CB, BA, BX, LAM, LN = 0, 16, 32, 48, 75, 83, 91, 99, 107, 115, 147, 155, 163, 171, 179
NCOL_IN = 187
OMM, HW0, HA0, C1, C0, HBA, HBX, CA, CA2 = 187, 214, 222, 230, 238, 246, 254, 262, 270
NCOL = 288

ENGS = ("pe", "act", "dve", "pool", "sp")


class Prog:
    def __init__(self, nc):
        self.nc = nc
        self.ops = {e: [] for e in ENGS}
        self.last_w = {}
        self.readers = {}
        self.dma_cnt = {}

    def op(self, eng, fn, reads=(), writes=(), dma=None, after=()):
        deps = []
        for w in after:
            t = self.last_w.get(w)
            if t is not None:
                deps.append(t)
            deps.extend(self.readers.get(w, ()))
        for r in reads:
            t = self.last_w.get(r)
            if t is not None:
                deps.append(t)
        for w in writes:
            t = self.last_w.get(w)
            if t is not None:
                deps.append(t)
            deps.extend(self.readers.get(w, ()))
        idx = len(self.ops[eng])
        if dma is not None:
            n = self.dma_cnt.get(dma, 0) + 1
            self.dma_cnt[dma] = n
            tok = ("d", dma, n)
        else:
            tok = ("c", eng, idx)
        dd = []
        seen = set()
        for t in deps:
            if t in seen:
                continue
            seen.add(t)
            if t[0] == "c" and t[1] == eng and eng == "pe":
                continue
            dd.append(t)
        self.ops[eng].append(dict(fn=fn, deps=dd, inc=False, dma=dma))
        for t in dd:
            if t[0] == "c":
                self.ops[t[1]][t[2]]["inc"] = True
        for r in reads:
            self.readers.setdefault(r, []).append(tok)
        for w in writes:
            self.last_w[w] = tok
            self.readers[w] = []
        return tok

    def emit(self, st, final_tokens):
        nc = self.nc
        cum = {}
        for e in ENGS:
            c = 0
            arr = []
            for o in self.ops[e]:
                if o["inc"] and o["dma"] is None:
                    c += 1
                arr.append(c)
            cum[e] = arr
        esem = {e: st.enter_context(nc.semaphore("s_" + e)) for e in ENGS if e != "sp"}
        dsem = {k: st.enter_context(nc.semaphore("d_%d" % i)) for i, k in enumerate(self.dma_cnt)}
        block = st.enter_context(nc.Block())

        def cond(t):
            if t[0] == "c":
                return esem[t[1]], cum[t[1]][t[2]], ("c", t[1])
            return dsem[t[1]], 16 * t[2], ("d", t[1])

        def run(e, engobj, extra=None):
            waited = {}

            def w(t):
                s, v, k = cond(t)
                if waited.get(k, 0) >= v:
                    return
                engobj.wait_ge(s, v)
                waited[k] = v

            for o in self.ops[e]:
                for t in o["deps"]:
                    w(t)
                ins = o["fn"](engobj)
                if o["dma"] is not None:
                    ins.then_inc(dsem[o["dma"]], 16)
                elif o["inc"]:
                    ins.then_inc(esem[e], 1)
            for t in (extra or ()):
                w(t)

        @block.tensor
        def _(eng):
            run("pe", eng)

        @block.scalar
        def _(eng):
            run("act", eng)

        @block.vector
        def _(eng):
            run("dve", eng)

        @block.gpsimd
        def _(eng):
            run("pool", eng)

        @block.sync
        def _(eng):
            run("sp", eng, final_tokens)


def build(NT, dbg=False, lim=99, tiny=False):
    nc = bass.Bass("TRN2", target_bir_lowering=False)
    di = lambda name, shape: nc.dram_tensor(name, shape, F32, kind="ExternalInput").ap()
    seq = NT * T if tiny else SEQ
    dff_, din_, dm_ = (256, 256, 256) if tiny else (DFF, DIN, D)
    cs = (lambda a, b: slice(0, b - a)) if tiny else (lambda a, b: slice(a, b))
    x_d = di("x", [seq, D])
    wg_d = [di("wg1", [D, dff_]), di("wg2", [D, dff_])]
    wu_d = [di("wu1", [D, dff_]), di("wu2", [D, dff_])]
    wd_d = [di("wd1", [dff_, D]), di("wd2", [dff_, D])]
    win_d = di("w_in", [D, din_])
    wout_d = di("w_out", [dm_, D])
    colp_d = di("colp", [128, NCOL_IN])
    w2p_d = di("w2p", [128, 1024])
    a2p_d = di("a2p", [128, 1024])
    g2_d = di("g2", [160, 1024])
    wabd_d = di("wabd", [128, 8, 128])
    wxbd_d = di("wxbd", [128, 8, 128])
    lnw_d = di("lnwb", [128, 1024])
    lnb_d = di("lnbb", [128, 1024])
    fn_d = di("fnb", [128, D])
    out_d = nc.dram_tensor("out", [seq, D], F32, kind="ExternalOutput").ap()

    st = contextlib.ExitStack()
    with st:
        sb = lambda name, shape, dt: st.enter_context(nc.sbuf_tensor("sb_" + name, shape, dt))
        h = sb("h", [128, NTB, D], F32)
        xnT = sb("xnT", [128, 16, T], BF16)
        act = sb("act", [128, 44, T], BF16)
        wA = [sb("wA%d" % i, [128, 16, 256], BF16) for i in range(3)]
        wB = [sb("wB%d" % i, [128, 2, D], BF16) for i in range(2)]
        colp = sb("colp", [128, NCOL], F32)
        w2p = sb("w2ps", [128, 1024], BF16)
        a2p = sb("a2ps", [128, 1024], BF16)
        g2a = sb("g2a", [128, 1024], BF16)
        g2b = sb("g2b", [128, 1024], BF16)
        wabd = sb("wabds", [128, 8, 128], BF16)
        wxbd = sb("wxbds", [128, 8, 128], BF16)
        lnwb = sb("lnwbs", [128, 1024], F32)
        lnbb = sb("lnbbs", [128, 1024], F32)
        fnb = sb("fnbs", [128, D], F32)
        ident = sb("ident", [128, 128], BF16)
        bones = sb("bones", [128, 128], BF16)
        ones = sb("ones", [128, 128], BF16)
        hind = sb("hind", [128, 2], BF16)
        mU = sb("mU", [128, 4, 128], BF16)
        mUi = sb("mUi", [128, 4, 128], BF16)
        mL = sb("mL", [128, 4, 128], BF16)
        rmask = sb("rmask", [128, T], F32)
        xnb = sb("xnb", [128, D], BF16)
        stat = sb("stat", [128, 64], F32)
        carry = sb("carry", [128, 32], F32)
        tsh = [sb("tsh%d" % i, [128, 4 + T], F32) for i in range(2)]
        lorab = sb("lorab", [128, T], BF16)
        sgxa = sb("sgxa", [128, T], BF16)
        sgxb = sb("sgxb", [128, T], BF16)
        xbuf = sb("xbuf", [128, 8, 4 + T], BF16)
        ylb = sb("ylb", [128, 8, T], BF16)
        hcar = sb("hcar", [128, 8], F32)
        vT = sb("vT", [128, 8, T], BF16)
        PC = sb("PC", [128, 8, NTB], F32)
        rkS = sb("rkS", [128, NTB, 16], F32)
        lrs = sb("lrs", [128, T], F32)
        tmA = sb("tmA", [128, 1024], BF16)
        tmBk = sb("tmBk", [128, 1024], BF16)
        tmK = sb("tmK", [128, 1024], BF16)
        tmV = sb("tmV", [128, 1024], BF16)
        AkT = sb("AkT", [128, 16, 128], BF16)
        ArbT = sb("ArbT", [128, 16, 128], BF16)
        ArkT = sb("ArkT", [128, 16, 128], BF16)
        Pm = [sb("Pm%d" % i, [128, 4, 128], BF16) for i in range(2)]
        Qm = [sb("Qm%d" % i, [128, 4, 128], BF16) for i in range(2)]
        Xf = sb("Xf", [128, 4, 128], F32)
        Xb = sb("Xb", [128, 4, 128], BF16)
        AwT = sb("AwT", [128, 8, 128], BF16)
        Uv = sb("Uv", [128, 1024], BF16)
        Ub = sb("Ub", [128, 1024], BF16)
        Hf = sb("Hf", [128, 8, 64], F32)
        Hb = sb("Hb", [128, 8, 2, 64], BF16)
        ybuf = sb("ybuf", [128, 1024], F32)
        yt1 = sb("yt1", [128, 1024], BF16)
        yg = sb("yg", [128, 1024], BF16)
        NTF, NTBF = 18, 6
        tpf_all = sb("tpf", [128, NTF, T], F32)
        tpf = [tpf_all[:, i, :] for i in range(NTF)]
        vbf = lambda i: tpf_all[:, i, :].bitcast(BF16).rearrange("p (a b) -> p a b", a=4)
        SETS = [
            dict(Pm=[Pm[0][:], Pm[1][:]], Qm=[Qm[0][:], Qm[1][:]], Xf=Xf[:], Xb=Xb[:],
                 kP=[[("Pm", 0)], [("Pm", 1)]], kQ=[[("Qm", 0)], [("Qm", 1)]], kXf=["Xf"], kXb=["Xb"]),
            dict(Pm=[vbf(0), vbf(1)], Qm=[vbf(2), vbf(3)], Xb=vbf(4),
                 Xf=tpf_all[:, 5:7, :].rearrange("p a (b c) -> p (a b) c", c=128),
                 kP=[[("tpf", 0)], [("tpf", 1)]], kQ=[[("tpf", 2)], [("tpf", 3)]], kXf=[("tpf", 5), ("tpf", 6)], kXb=[("tpf", 4)]),
        ]
        tpb = [sb("tpb%d" % i, [128, T], BF16) for i in range(NTBF)]
        PS = [st.enter_context(nc.psum_tensor("ps%d" % i, [128, 512], F32)) for i in range(8)]
        PSB = [p[:].bitcast(BF16) for p in PS]

        P = Prog(nc)
        cnt = dict(f=0, b=0, bank=0, wA=0, wB=0, tsh=0, res=set())

        def tmpf():
            i = cnt["f"] % NTF
            cnt["f"] += 1
            return tpf[i], ("tpf", i)

        def tmpb():
            i = cnt["b"] % NTBF
            cnt["b"] += 1
            return tpb[i], ("tpb", i)

        def nb(reserve=False):
            while True:
                i = cnt["bank"] % 8
                cnt["bank"] += 1
                if i not in cnt["res"]:
                    break
            if reserve:
                cnt["res"].add(i)
            return i

        def nwA():
            i = cnt["wA"] % 3
            cnt["wA"] += 1
            return i

        def nwB():
            i = cnt["wB"] % 2
            cnt["wB"] += 1
            return i

        def MM(out, lhsT, rhs, st_, sp_, R, W):
            P.op("pe", lambda e: e.matmul(out, lhsT=lhsT, rhs=rhs, start=st_, stop=sp_), R, W)

        def TR(out, in_, R, W):
            K = in_.shape[0]
            P.op("pe", lambda e: e.transpose(out=out, in_=in_, identity=ident[0:K, 0:K]), list(R) + ["ident"], W)

        def ACT(out, in_, func, R, W, scale=1.0, bias=None, accum=None):
            kw = {}
            if bias is not None:
                kw["bias"] = bias
            if accum is not None:
                kw["accum_out"] = accum
            P.op("act", lambda e: e.activation(out=out, in_=in_, func=func, scale=scale, **kw), R, W)

        def TT(out, in0, in1, op, R, W, eng="dve"):
            P.op(eng, lambda e: e.tensor_tensor(out=out, in0=in0, in1=in1, op=op), R, W)

        def TS(out, in0, s1, s2, op0, op1, R, W, eng="dve"):
            if s2 is None:
                P.op(eng, lambda e: e.tensor_scalar(out=out, in0=in0, scalar1=s1, scalar2=None, op0=op0), R, W)
            else:
                P.op(eng, lambda e: e.tensor_scalar(out=out, in0=in0, scalar1=s1, scalar2=s2, op0=op0, op1=op1), R, W)

        def STT(out, in0, sc, in1, op0, op1, R, W):
            P.op("dve", lambda e: e.scalar_tensor_tensor(out=out, in0=in0, scalar=sc, in1=in1, op0=op0, op1=op1), R, W)

        def CP(out, in_, R, W, eng="dve"):
            P.op(eng, lambda e: e.tensor_copy(out=out, in_=in_), R, W)

        def RCP(out, in_, R, W):
            P.op("dve", lambda e: e.reciprocal(out=out, in_=in_), R, W)

        def SCAN(out, d0, d1, init, R, W):
            P.op("dve", lambda e: e.tensor_tensor_scan(out=out, data0=d0, data1=d1, initial=init, op0=ALU.mult, op1=ALU.add), R, W)

        def RSUM(out, in_, R, W):
            P.op("dve", lambda e: e.tensor_reduce(out=out, in_=in_, axis=AX.X, op=ALU.add), R, W)

        def DMA(eng, out, in_, R, W, key, after=()):
            return P.op(eng, lambda e: e.dma_start(out=out, in_=in_), R, W, dma=key, after=after)

        NBLK = 180
        scr = nc.dram_tensor("wscr", [NBLK, 128, 4096], BF16, kind="Internal").ap()
        cur = {"ti": 0, "blk": 0}
        st_tokens = []

        def WL(buf, wkeys, key, pieces):
            b = cur["blk"]
            cur["blk"] += 1
            assert b < NBLK
            flat = buf[:].rearrange("p a b -> p (a b)")
            if cur["ti"] == 0:
                for i, (dst, src) in enumerate(pieces):
                    if i == len(pieces) - 1:
                        DMA("pool", dst, src, [], wkeys, key)
                    else:
                        DMA("pool", dst, src, [], [], key, after=wkeys)
                st_tokens.append(DMA("sp", scr[b], flat, wkeys, [("scr", b)], ("scrst", key)))
            else:
                if key[0] == "wB":
                    DMA("pool", flat, scr[b], [("scr", b)], wkeys, key)
                else:
                    DMA("sp", flat, scr[b], [("scr", b)], wkeys, ("hw", key))

        def run2(gens):
            active = list(gens)
            while active:
                for g in list(active):
                    try:
                        next(g)
                    except StopIteration:
                        active.remove(g)

        def MSET(ap, val, W):
            P.op("pool", lambda e: e.memset(ap, val), [], W)

        def ASEL(ap, pattern, cmp, base, cm, W):
            P.op("pool", lambda e: e.affine_select(out=ap, in_=ap, pattern=pattern, compare_op=cmp, fill=0.0, base=base, channel_multiplier=cm), W, W)

        col = lambda c0, n=1: colp[:, c0:c0 + n]

        DMA("sp", colp[:, 0:NCOL_IN], colp_d, [], ["colp"], "colp")
        DMA("sp", lnwb[:], lnw_d, [], ["lnwb"], "lnwb")
        DMA("sp", lnbb[:], lnb_d, [], ["lnbb"], "lnbb")
        DMA("sp", fnb[:], fn_d, [], ["fnb"], "fnb")
        DMA("pool", w2p[:], w2p_d, [], ["w2p"], "w2p")
        DMA("pool", a2p[:], a2p_d, [], ["a2p"], "a2p")
        DMA("pool", g2a[:], g2_d[0:128, :], [], ["g2a"], "g2a")
        MSET(g2b[:], 0.0, ["g2b"])
        MSET(sgxb[:], 0.0, ["sgxb"])
        DMA("pool", g2b[0:32, :], g2_d[128:160, :], [], ["g2b"], "g2b")
        DMA("pool", wabd[:], wabd_d, [], ["wabd"], "wabd")
        DMA("pool", wxbd[:], wxbd_d, [], ["wxbd"], "wxbd")
        MSET(ident[:], 1.0, ["ident"])
        ASEL(ident[:], [[-1, 128]], ALU.is_equal, 0, 1, ["ident"])
        MSET(ones[:], 1.0, ["ones"])
        MSET(bones[:], 1.0, ["bones"])
        MSET(bones[0:64, 64:128], 0.0, ["bones"])
        MSET(bones[64:128, 0:64], 0.0, ["bones"])
        MSET(hind[:], 0.0, ["hind"])
        MSET(hind[0:64, 0:1], 1.0, ["hind"])
        MSET(hind[64:128, 1:2], 1.0, ["hind"])
        MSET(mU[:], 1.0, ["mU"])
        ASEL(mU[:], [[0, 4], [1, 128]], ALU.is_gt, 0, -1, ["mU"])
        MSET(mUi[:], 1.0, ["mUi"])
        ASEL(mUi[:], [[0, 4], [1, 128]], ALU.is_ge, 0, -1, ["mUi"])
        MSET(mL[:], 1.0, ["mL"])
        ASEL(mL[:], [[0, 4], [-1, 128]], ALU.is_gt, 0, 1, ["mL"])
        MSET(rmask[:], 1.0, ["rmask"])
        MSET(rmask[:, 0:1], 0.0, ["rmask"])
        MSET(rmask[:, 128:129], 0.0, ["rmask"])
        MSET(carry[:], 0.0, ["carry"])
        MSET(xbuf[:], 0.0, ["xbuf"])
        MSET(hcar[:], 0.0, ["hcar"])
        MSET(Hf[:], 0.0, ["Hf"])
        MSET(Hb[:], 0.0, ["Hb"])
        MSET(stat[:], 0.0, ["stat"])
        TS(col(OMM, 27), col(MU, 27), -1.0, 1.0, ALU.mult, ALU.add, ["colp"], ["colp"])
        TS(col(HW0, 8), col(W0, 8), 0.5, None, ALU.mult, None, ["colp"], ["colp"])
        TS(col(HA0, 8), col(A0, 8), 0.5, None, ALU.mult, None, ["colp"], ["colp"])
        TS(col(C1, 8), col(KA, 8), 0.5, None, ALU.mult, None, ["colp"], ["colp"])
        TS(col(C0, 8), col(KA, 8), -0.5, 1.0, ALU.mult, ALU.add, ["colp"], ["colp"])
        TS(col(HBA, 8), col(BA, 8), 0.5, None, ALU.mult, None, ["colp"], ["colp"])
        TS(col(HBX, 8), col(BX, 8), 0.5, None, ALU.mult, None, ["colp"], ["colp"])
        ev = stat[:, 32:40]
        pv = stat[:, 40:48]
        ACT(ev, col(LAM, 8), AF.Exp, ["colp"], ["stat"], scale=-1.0)
        TS(pv, ev, 0.2, -0.25, ALU.mult, ALU.add, ["stat"], ["stat"])
        TT(pv, pv, ev, ALU.mult, ["stat"], ["stat"])
        TS(pv, pv, 1.0 / 3.0, None, ALU.add, None, ["stat"], ["stat"])
        TT(pv, pv, ev, ALU.mult, ["stat"], ["stat"])
        TS(pv, pv, -0.5, None, ALU.add, None, ["stat"], ["stat"])
        TT(pv, pv, ev, ALU.mult, ["stat"], ["stat"])
        TS(pv, pv, 1.0, None, ALU.add, None, ["stat"], ["stat"])
        TT(pv, pv, ev, ALU.mult, ["stat"], ["stat"])
        TS(col(CA, 8), pv, -4.0, None, ALU.mult, None, ["stat"], ["colp"])
        TS(col(CA2, 8), pv, -8.0, None, ALU.mult, None, ["stat"], ["colp"])

        wgv = [w.rearrange("(kc p) n -> p kc n", p=128) for w in wg_d]
        wuv = [w.rearrange("(kc p) n -> p kc n", p=128) for w in wu_d]
        wdv = [w.rearrange("(hc p) n -> p hc n", p=128) for w in wd_d]
        winv = win_d.rearrange("(kc p) n -> p kc n", p=128)
        woutv = wout_d.rearrange("(kc p) n -> p kc n", p=128)
        xv = x_d.rearrange("(n p) d -> n p d", p=128)
        ov = out_d.rearrange("(n p) d -> n p d", p=128)

        def rstd_of(ss_ap, n, inv_n, eps, key="stat"):
            ms = stat[:, 16:16 + n]
            TS(ms, ss_ap, inv_n, eps, ALU.mult, ALU.add, [key], ["stat"])
            ACT(ms, ms, AF.Sqrt, ["stat"], ["stat"])
            rs = stat[:, 0:n]
            RCP(rs, ms, ["stat"], ["stat"])
            return rs

        def norm_stage(gcol):
            for tb in range(NTB):
                ss = stat[:, 48:49]
                ACT(xnb[:], h[:, tb, :], AF.Square, [("h", tb)], ["xnb", "stat"], accum=ss)
                rs = rstd_of(ss, 1, 1.0 / D, 1e-6)
                ACT(xnb[:], h[:, tb, :], AF.Identity, [("h", tb), "stat"], ["xnb"], scale=rs)
                for half in range(2):
                    bk = nb()
                    for j in range(8):
                        kc = half * 8 + j
                        TR(PSB[bk][:, j * 128:(j + 1) * 128], xnb[:, kc * 128:(kc + 1) * 128], ["xnb"], [("ps", bk)])
                    TT(xnT[:, half * 8:(half + 1) * 8, tb * 128:(tb + 1) * 128],
                       PSB[bk][:, 0:1024].rearrange("p (a b) -> p a b", a=8),
                       colp[:, gcol + half * 8:gcol + half * 8 + 8].unsqueeze(2).to_broadcast([128, 8, 128]),
                       ALU.mult, [("ps", bk), "colp"], [("xnT", kc) for kc in range(half * 8, half * 8 + 8)])

        def ffn_stage(l):
            for b in range(22):
                sg_, su_ = nwA(), nwA()
                WL(wA[sg_], [("wA", sg_, 0), ("wA", sg_, 1)], ("wA", sg_), [(wA[sg_][:], wgv[l][:, :, cs(b * 256, (b + 1) * 256)])])
                WL(wA[su_], [("wA", su_, 0), ("wA", su_, 1)], ("wA", su_), [(wA[su_][:], wuv[l][:, :, cs(b * 256, (b + 1) * 256)])])
                for j in range(2):
                    hc = 2 * b + j
                    bk = nb()
                    for kc in range(16):
                        MM(PS[bk][:, 0:T], wA[sg_][:, kc, j * 128:(j + 1) * 128], xnT[:, kc, :], kc == 0, kc == 15,
                           [("wA", sg_, j), ("xnT", kc)], [("ps", bk)])
                    for kc in range(16):
                        MM(PS[bk][:, 256:256 + T], wA[su_][:, kc, j * 128:(j + 1) * 128], xnT[:, kc, :], kc == 0, kc == 15,
                           [("wA", su_, j), ("xnT", kc)], [("ps", bk)])
                    th, kth = tmpf()
                    ACT(th[:], PS[bk][:, 0:T], AF.Tanh, [("ps", bk)], [kth], scale=0.5)
                    t1, kt1 = tmpf()
                    STT(t1[:], th[:], 1.0, PS[bk][:, 0:T], ALU.add, ALU.mult, [kth, ("ps", bk)], [kt1])
                    STT(act[:, hc, :], t1[:], 0.5, PS[bk][:, 256:256 + T], ALU.mult, ALU.mult, [kt1, ("ps", bk)], [("act", hc)])
            for b in range(22):
                s = nwB()
                WL(wB[s], [("wB", s, 0), ("wB", s, 1)], ("wB", s), [(wB[s][:], wdv[l][:, cs(2 * b, 2 * b + 2), :])])
                for j in range(2):
                    hc = 2 * b + j
                    for tb in range(NTB):
                        for db in range(4):
                            MM(PS[tb * 4 + db][:, :], act[:, hc, tb * 128:(tb + 1) * 128], wB[s][:, j, db * 512:(db + 1) * 512],
                               hc == 0, hc == 43, [("act", hc), ("wB", s, j)], [("ps", tb * 4 + db)])
            for tb in range(NTB):
                for db in range(4):
                    hs = h[:, tb, db * 512:(db + 1) * 512]
                    STT(hs, PS[tb * 4 + db][:, :], 0.5, hs, ALU.mult, ALU.add, [("ps", tb * 4 + db), ("h", tb)], [("h", tb)])

        def proj(slot, half, M):
            bk = nb()
            for kc in range(16):
                MM(PS[bk][0:M, 0:T], wA[slot][:, kc, half * 128:half * 128 + M], xnT[:, kc, :], kc == 0, kc == 15,
                   [("wA", slot, half), ("xnT", kc)], [("ps", bk)])
            return bk

        def shift_evac(oc, bk, M, out, okey):
            i = cnt["tsh"] % 2
            cnt["tsh"] += 1
            ts_ = tsh[i]
            k = ("tsh", i)
            ACT(ts_[0:M, 3:4], carry[0:M, oc:oc + 1], AF.Copy, ["carry"], [k])
            ACT(ts_[0:M, 4:4 + T], PS[bk][0:M, 0:T], AF.Identity, [("ps", bk), "colp"], [k], scale=colp[0:M, MU + oc:MU + oc + 1])
            ACT(carry[0:M, oc:oc + 1], ts_[0:M, 3 + T:4 + T], AF.Copy, [k], ["carry"])
            STT(out, PS[bk][0:M, 0:T], colp[0:M, OMM + oc:OMM + oc + 1], ts_[0:M, 3:3 + T], ALU.mult, ALU.add,
                [("ps", bk), k, "colp"], [okey])

        def mixer_stage(lim=99):
            s0 = nwA()
            WL(wA[s0], [("wA", s0, 0), ("wA", s0, 1)], ("wA", s0),
               [(wA[s0][:, :, 0:64], winv[:, :, cs(1024, 1088)]), (wA[s0][:, :, 64:128], winv[:, :, cs(3136, 3200)]),
                (wA[s0][:, :, 128:256], winv[:, :, cs(3200, 3328)])])
            s1 = nwA()
            WL(wA[s1], [("wA", s1, 0), ("wA", s1, 1)], ("wA", s1), [(wA[s1][:, :, 0:32], winv[:, :, cs(3328, 3360)])])
            bk = proj(s0, 0, 128)
            t0, k0 = tmpf()
            shift_evac(24, bk, 128, t0[:], k0)
            ACT(lorab[0:64, :], t0[0:64, :], AF.Tanh, [k0], ["lorab"])
            ACT(lorab[64:128, :], t0[64:128, :], AF.Copy, [k0], ["lorab"])
            bk = proj(s0, 1, 128)
            t0, k0 = tmpf()
            shift_evac(25, bk, 128, t0[:], k0)
            t1, k1 = tmpf()
            ACT(t1[:], t0[:], AF.Tanh, [k0], [k1], scale=0.5)
            TS(sgxa[:], t1[:], 0.5, 0.5, ALU.mult, ALU.add, [k1], ["sgxa"])
            bk = proj(s1, 0, 32)
            t0, k0 = tmpf()
            shift_evac(26, bk, 32, t0[0:32, :], k0)
            t1, k1 = tmpf()
            ACT(t1[0:32, :], t0[0:32, :], AF.Tanh, [k0], [k1], scale=0.5)
            TS(sgxb[0:32, :], t1[0:32, :], 0.5, 0.5, ALU.mult, ALU.add, [k1], ["sgxb"])

            if lim < 4.1:
                return
            ACT(xbuf[:, :, 1:4], xbuf[:, :, 1 + T:4 + T], AF.Copy, ["xbuf"] + [("xb", c) for c in range(8)], ["xbuf"])
            bsum = nb(True)
            def lru_chunk(c, sx, sgt):
                bk = proj(sx, c % 2, 128)
                yield
                ACT(xbuf[:, c, 4:4 + T], PS[bk][:, 0:T], AF.Copy, [("ps", bk), "xbuf"], [("xb", c)])
                yield
                bk = proj(sgt, c % 2, 128)
                yield
                gt = act[:, 32 + c, :]
                ACT(gt, PS[bk][:, 0:T], AF.Copy, [("ps", bk)], [("act", 32 + c)])
                yield
                if lim < 4.11:
                    return
                xc, kxc = tmpf()
                TS(xc[:], xbuf[:, c, 1:1 + T], col(CW + c), col(CB + c), ALU.mult, ALU.add, [("xb", c), "xbuf", "colp"], [kxc])
                yield
                for jj in (1, 2, 3):
                    STT(xc[:], xbuf[:, c, 1 + jj:1 + jj + T], col(CW + jj * 8 + c), xc[:], ALU.mult, ALU.add, [("xb", c), "xbuf", kxc, "colp"], [kxc])
                    yield
                xcb, kxcb = tmpb()
                ACT(xcb[:], xc[:], AF.Copy, [kxc], [kxcb])
                yield
                bk = nb()
                MM(PS[bk][:, 0:T], wabd[:, c, :], xcb[:], True, True, ["wabd", kxcb], [("ps", bk)])
                yield
                MM(PS[bk][:, 256:256 + T], wxbd[:, c, :], xcb[:], True, True, ["wxbd", kxcb], [("ps", bk)])
                yield
                thr, kthr = tmpf()
                ACT(thr[:], PS[bk][:, 0:T], AF.Tanh, [("ps", bk), "colp"], [kthr], scale=0.5, bias=col(HBA + c))
                yield
                thi, kthi = tmpf()
                ACT(thi[:], PS[bk][:, 256:256 + T], AF.Tanh, [("ps", bk), "colp"], [kthi], scale=0.5, bias=col(HBX + c))
                yield
                if lim < 4.12:
                    return
                a_, ka_ = tmpf()
                ACT(a_[:], thr[:], AF.Exp, [kthr, "colp"], [ka_], scale=col(CA + c), bias=col(CA + c))
                yield
                a2, ka2 = tmpf()
                ACT(a2[:], thr[:], AF.Exp, [kthr, "colp"], [ka2], scale=col(CA2 + c), bias=col(CA2 + c))
                yield
                TS(a2[:], a2[:], -1.0, 1.0, ALU.mult, ALU.add, [ka2], [ka2])
                yield
                TS(a2[:], a2[:], 0.0, None, ALU.max, None, [ka2], [ka2])
                yield
                ACT(a2[:], a2[:], AF.Sqrt, [ka2], [ka2])
                yield
                ui, kui = tmpf()
                STT(ui[:], thi[:], 1.0, xc[:], ALU.add, ALU.mult, [kthi, kxc], [kui])
                yield
                STT(ui[:], ui[:], 0.5, a2[:], ALU.mult, ALU.mult, [kui, ka2], [kui])
                yield
                if lim < 4.13:
                    return
                hs, khs = tmpf()
                SCAN(hs[:], a_[:], ui[:], hcar[:, c:c + 1], [ka_, kui, "hcar"], [khs])
                yield
                CP(hcar[:, c:c + 1], hs[:, T - 1:T], [khs], ["hcar"])
                yield
                if lim < 4.14:
                    return
                x2, kx2 = tmpf()
                ACT(x2[:], gt, AF.Square, [("act", 32 + c)], [kx2])
                yield
                TS(x2[:], x2[:], 0.044715, 1.0, ALU.mult, ALU.add, [kx2], [kx2])
                yield
                TT(x2[:], x2[:], gt, ALU.mult, [kx2, ("act", 32 + c)], [kx2])
                yield
                ACT(x2[:], x2[:], AF.Tanh, [kx2], [kx2], scale=0.7978845608028654)
                yield
                STT(x2[:], x2[:], 1.0, gt, ALU.add, ALU.mult, [kx2, ("act", 32 + c)], [kx2])
                yield
                STT(ylb[:, c, :], x2[:], 0.5, hs[:], ALU.mult, ALU.mult, [kx2, khs], [("ylb", c)])
                yield
                if lim < 4.15:
                    return
                sq, ksq = tmpb()
                ACT(sq[:], ylb[:, c, :], AF.Square, [("ylb", c)], [ksq])
                yield
                MM(PS[bsum][:, 0:T], ones[:], sq[:], c == 0, c == 7, ["ones", ksq], [("ps", bsum)])
                yield
            for cp in range(4):
                c = 2 * cp
                sx = nwA()
                WL(wA[sx], [("wA", sx, 0), ("wA", sx, 1)], ("wA", sx), [(wA[sx][:], winv[:, :, cs(3360 + c * 128, 3360 + c * 128 + 256)])])
                sgt = nwA()
                WL(wA[sgt], [("wA", sgt, 0), ("wA", sgt, 1)], ("wA", sgt), [(wA[sgt][:], winv[:, :, cs(4384 + c * 128, 4384 + c * 128 + 256)])])
                run2([lru_chunk(c, sx, sgt), lru_chunk(c + 1, sx, sgt)])
            cnt["res"].discard(bsum)
            if lim < 4.16:
                return
            rs_, krs = tmpf()
            TS(rs_[:], PS[bsum][:, 0:T], 1.0 / 1024.0, 1e-6, ALU.mult, ALU.add, [("ps", bsum)], [krs])
            if lim < 4.17:
                return
            ACT(rs_[:], rs_[:], AF.Sqrt, [krs], [krs])
            if lim < 4.18:
                return
            RCP(lrs[:], rs_[:], [krs], ["lrs"])

            if lim < 4.2:
                return
            brk = nb(True)
            def rwkv_pair(c):
                sr = nwA()
                WL(wA[sr], [("wA", sr, 0), ("wA", sr, 1)], ("wA", sr),
                   [(wA[sr][:, :, 0:128], winv[:, :, cs(c * 128, (c + 1) * 128)]),
                    (wA[sr][:, :, 128:256], winv[:, :, cs(1088 + c * 128, 1088 + (c + 1) * 128)])])
                yield
                bk = proj(sr, 0, 128)
                yield
                rp, krp = tmpf()
                shift_evac(c, bk, 128, rp[:], krp)
                yield
                bk = proj(sr, 1, 128)
                yield
                kp, kkp = tmpf()
                shift_evac(8 + c, bk, 128, kp[:], kkp)
                yield
                sv = sr
                WL(wA[sv], [("wA", sv, 0), ("wA", sv, 1)], ("wA", sv), [(wA[sv][:, :, 0:128], winv[:, :, cs(2112 + c * 128, 2112 + (c + 1) * 128)])])
                yield
                bk = proj(sv, 0, 128)
                yield
                shift_evac(16 + c, bk, 128, vT[:, c, :], ("vT", c))
                yield
                if lim < 4.21:
                    return
                bk = nb()
                MM(PS[bk][:, 0:T], w2p[:, c * 128:(c + 1) * 128], lorab[:, :], True, True, ["w2p", "lorab"], [("ps", bk)])
                yield
                MM(PS[bk][:, 256:256 + T], a2p[:, c * 128:(c + 1) * 128], lorab[:, :], True, True, ["a2p", "lorab"], [("ps", bk)])
                yield
                S_, kS = tmpf()
                ACT(S_[:], PS[bk][:, 0:T], AF.Tanh, [("ps", bk), "colp"], [kS], scale=0.5, bias=col(HW0 + c))
                yield
                TS(S_[:], S_[:], 1.0, None, ALU.add, None, [kS], [kS])
                yield
                tha, ktha = tmpf()
                ACT(tha[:], PS[bk][:, 256:256 + T], AF.Tanh, [("ps", bk), "colp"], [ktha], scale=0.5, bias=col(HA0 + c))
                yield
                Lc, kLc = tmpf()
                SCAN(Lc[:], rmask[:], S_[:], 0.0, ["rmask", kS], [kLc])
                yield
                eL, keL = tmpf()
                ACT(eL[:], Lc[:], AF.Exp, [kLc], [keL], scale=-CD)
                yield
                TT(act[:, 24 + c, :], rp[:], eL[:], ALU.mult, [krp, keL], [("act", 24 + c)])
                yield
                for n in range(NTB):
                    CP(PC[:, c, n:n + 1], eL[:, n * 128 + 127:n * 128 + 128], [keL], [("PC", n)])
                    yield
                TT(S_[:], Lc[:], S_[:], ALU.subtract, [kLc, kS], [kS])
                yield
                ACT(S_[:], S_[:], AF.Exp, [kS], [kS], scale=-CD)
                yield
                ACT(Lc[:], Lc[:], AF.Exp, [kLc], [kLc], scale=CD)
                yield
                if lim < 4.22:
                    return
                kks, kkks = tmpf()
                TS(kks[:], kp[:], col(KK + c), None, ALU.mult, None, [kkp, "colp"], [kkks])
                yield
                sq, ksq = tmpb()
                ACT(sq[:], kks[:], AF.Square, [kkks], [ksq])
                yield
                bk2 = nb()
                MM(PS[bk2][:, 0:T], bones[:], sq[:], True, True, ["bones", ksq], [("ps", bk2)])
                yield
                nr, knr = tmpf()
                ACT(nr[:], PS[bk2][:, 0:T], AF.Sqrt, [("ps", bk2)], [knr])
                yield
                TS(nr[:], nr[:], 1e-12, None, ALU.max, None, [knr], [knr])
                yield
                rn, krn = tmpf()
                RCP(rn[:], nr[:], [knr], [krn])
                yield
                TT(kks[:], kks[:], rn[:], ALU.mult, [kkks, krn], [kkks])
                yield
                if lim < 4.23:
                    return
                STT(act[:, c, :], kks[:], -1.0, S_[:], ALU.mult, ALU.mult, [kkks, kS], [("act", c)])
                yield
                TS(rn[:], tha[:], 0.5, 0.5, ALU.mult, ALU.add, [ktha], [krn])
                yield
                TT(rn[:], rn[:], kks[:], ALU.mult, [krn, kkks], [krn])
                yield
                TT(act[:, 8 + c, :], rn[:], Lc[:], ALU.mult, [krn, kLc], [("act", 8 + c)])
                yield
                TS(tha[:], tha[:], col(C1 + c), col(C0 + c), ALU.mult, ALU.add, [ktha, "colp"], [ktha])
                yield
                TT(tha[:], tha[:], kp[:], ALU.mult, [ktha, kkp], [ktha])
                yield
                TT(act[:, 16 + c, :], tha[:], Lc[:], ALU.mult, [ktha, kLc], [("act", 16 + c)])
                yield
                if lim < 4.24:
                    return
                rkb, krkb = tmpb()
                STT(rkb[:], rp[:], col(RK + c), tha[:], ALU.mult, ALU.mult, [krp, ktha, "colp"], [krkb])
                yield
                for n in range(NTB):
                    MM(PS[brk][:, n * 16 + 2 * c:n * 16 + 2 * c + 2], rkb[:, n * 128:(n + 1) * 128], hind[:], True, True,
                       [krkb, "hind"], [("ps", brk)])
                    yield
            for cp in range(4):
                run2([rwkv_pair(2 * cp), rwkv_pair(2 * cp + 1)])
            if lim < 4.25:
                cnt["res"].discard(brk)
                return
            CP(rkS[:].rearrange("p a b -> p (a b)"), PS[brk][:, 0:NTB * 16], [("ps", brk)], ["rkS"])
            cnt["res"].discard(brk)
            for c in range(8):
                STT(xnT[:, 8 + c, :], ylb[:, c, :], col(LN + c), lrs[:], ALU.mult, ALU.mult, [("ylb", c), "lrs", "colp"], [("xnT", 8 + c)])

            if lim < 4.3:
                return
            for n in range(NTB):
                ts = slice(n * 128, (n + 1) * 128)
                for (base, src, dst, dk) in ((0, act, tmA, "tmA"), (8, act, tmBk, "tmBk"), (16, act, tmK, "tmK"), (0, vT, tmV, "tmV")):
                    bk = nb()
                    for c in range(8):
                        rk_ = ("vT", c) if src is vT else ("act", base + c)
                        TR(PSB[bk][:, c * 128:(c + 1) * 128], src[:, base + c, ts], [rk_], [("ps", bk)])
                    ACT(dst[:], PSB[bk][:, 0:1024], AF.Copy, [("ps", bk)], [dk])
                def inv_batch(hb4, S):
                    par, half = hb4 // 2, hb4 % 2
                    hs4 = [8 * half + par + 2 * i for i in range(4)]
                    hv = lambda t3: t3[:].rearrange("p (a two) s -> p a two s", two=2)[:, 4 * half:4 * half + 4, par, :]
                    cv = lambda t2: t2[:].rearrange("p (a two d) -> p a two d", two=2, d=64)[:, 4 * half:4 * half + 4, par, :]
                    Pm_, Qm_, Xf_, Xb_ = S["Pm"], S["Qm"], S["Xf"], S["Xb"]
                    kP, kQ, kXf, kXb = S["kP"], S["kQ"], S["kXf"], S["kXb"]

                    def hsl(base, hh):
                        c, r0 = hh // 2, (hh % 2) * 64
                        return act[r0:r0 + 64, base + c, ts], ("act", base + c)

                    def amat(lb, rb, mask, mkey, dst3, dkeys):
                        bk = nb()
                        for i, hh in enumerate(hs4):
                            l_, lk = hsl(lb, hh)
                            r_, rk2 = hsl(rb, hh)
                            MM(PS[bk][:, i * 128:(i + 1) * 128], l_, r_, True, True, [lk, rk2], [("ps", bk)])
                            yield
                        TT(dst3, PS[bk][:, :].rearrange("p (a b) -> p a b", a=4), mask[:], ALU.mult, [("ps", bk), mkey], dkeys)
                        yield

                    yield from amat(8, 0, mU, "mU", Pm_[0], kP[0])
                    yield from amat(0, 8, mL, "mL", Qm_[0], kQ[0])
                    yield from amat(16, 0, mU, "mU", hv(AkT), [("AkT", hb4)])
                    yield from amat(8, 24, mUi, "mUi", hv(ArbT), [("ArbT", hb4)])
                    yield from amat(16, 24, mUi, "mUi", hv(ArkT), [("ArkT", hb4)])
                    bk = nb()
                    for i, hh in enumerate(hs4):
                        MM(PS[bk][:, i * 128 + 64:i * 128 + 128], AkT[:, hh, :], tmV[:, hh * 64:(hh + 1) * 64], True, True,
                           [("AkT", hb4), "tmV"], [("ps", bk)])
                        yield
                    CP(Xf_[:, :, 64:128], PS[bk][:, :].rearrange("p (a b) -> p a b", a=4)[:, :, 64:128], [("ps", bk)], kXf)
                    yield
                    CP(Xf_[:, :, 0:64], cv(tmA), ["tmA"], kXf)
                    yield
                    ACT(Xb_, Xf_, AF.Copy, kXf, kXb)
                    yield
                    cur = 0
                    for lev in range(7):
                        bk = nb()
                        for i in range(4):
                            MM(PS[bk][:, i * 128:(i + 1) * 128], Pm_[cur][:, i, :], Xb_[:, i, :], True, True, kP[cur] + kXb, [("ps", bk)])
                            yield
                        TT(Xf_, Xf_, PS[bk][:, :].rearrange("p (a b) -> p a b", a=4), ALU.add, kXf + [("ps", bk)], kXf)
                        yield
                        ACT(Xb_, Xf_, AF.Copy, kXf, kXb)
                        yield
                        if lev < 6:
                            nxt = 1 - cur
                            bk = nb()
                            for i in range(4):
                                MM(PS[bk][:, i * 128:(i + 1) * 128], Qm_[cur][:, i, :], Pm_[cur][:, i, :], True, True, kP[cur] + kQ[cur], [("ps", bk)])
                                yield
                            bk2 = None
                            if lev < 5:
                                bk2 = nb()
                                for i in range(4):
                                    MM(PS[bk2][:, i * 128:(i + 1) * 128], Pm_[cur][:, i, :], Qm_[cur][:, i, :], True, True, kP[cur] + kQ[cur], [("ps", bk2)])
                                    yield
                            ACT(Pm_[nxt], PS[bk][:, :].rearrange("p (a b) -> p a b", a=4), AF.Copy, [("ps", bk)], kP[nxt])
                            yield
                            if bk2 is not None:
                                CP(Qm_[nxt], PS[bk2][:, :].rearrange("p (a b) -> p a b", a=4), [("ps", bk2)], kQ[nxt])
                                yield
                            cur = nxt
                    bk = nb()
                    r0 = par * 64
                    for i, hh in enumerate(hs4):
                        TR(PSB[bk][r0:r0 + 64, i * 128:(i + 1) * 128], Xb_[:, i, 0:64], kXb, [("ps", bk)])
                        yield
                    ACT(AwT[r0:r0 + 64, 4 * half:4 * half + 4, :], PSB[bk][r0:r0 + 64, 0:512].rearrange("p (a b) -> p a b", a=4), AF.Copy, [("ps", bk)], ["AwT"])
                    yield
                    CP(cv(Uv), Xf_[:, :, 64:128], kXf, ["Uv"])
                    yield

                run2([inv_batch(0, SETS[0]), inv_batch(1, SETS[1])])
                run2([inv_batch(2, SETS[0]), inv_batch(3, SETS[1])])
                bu = [nb(), nb()]
                for hh in range(16):
                    c, r0 = hh // 2, (hh % 2) * 64
                    MM(PS[bu[hh // 8]][:, (hh % 8) * 64:(hh % 8) * 64 + 64], AwT[:, c, :], Hb[:, c, hh % 2, :], True, True,
                       ["AwT", "Hb"], [("ps", bu[hh // 8])])
                for q in range(2):
                    TT(Ub[:, q * 512:(q + 1) * 512], PS[bu[q]][:, :], Uv[:, q * 512:(q + 1) * 512], ALU.add, [("ps", bu[q]), "Uv"], ["Ub"])
                by = [nb(), nb()]
                for hh in range(16):
                    c, r0 = hh // 2, (hh % 2) * 64
                    o = PS[by[hh // 8]][:, (hh % 8) * 64:(hh % 8) * 64 + 64]
                    MM(o, act[:, 24 + c, ts], Hb[:, c, hh % 2, :], True, False, [("act", 24 + c), "Hb"], [("ps", by[hh // 8])])
                    MM(o, ArbT[:, hh, :], Ub[:, hh * 64:(hh + 1) * 64], False, False, [("ArbT", (hh % 2) * 2 + hh // 8), "Ub"], [("ps", by[hh // 8])])
                    MM(o, ArkT[:, hh, :], tmV[:, hh * 64:(hh + 1) * 64], False, True, [("ArkT", (hh % 2) * 2 + hh // 8), "tmV"], [("ps", by[hh // 8])])
                bh = nb()
                for hh in range(16):
                    c, r0 = hh // 2, (hh % 2) * 64
                    o = PS[bh][r0:r0 + 64, c * 64:(c + 1) * 64]
                    MM(o, tmBk[:, hh * 64:(hh + 1) * 64], Ub[:, hh * 64:(hh + 1) * 64], True, False, ["tmBk", "Ub"], [("ps", bh)])
                    MM(o, tmK[:, hh * 64:(hh + 1) * 64], tmV[:, hh * 64:(hh + 1) * 64], False, True, ["tmK", "tmV"], [("ps", bh)])
                TT(Hf[:], Hf[:], PS[bh][:, :].rearrange("p (a b) -> p a b", a=8), ALU.add, ["Hf", ("ps", bh)], ["Hf"])
                TT(Hf[:], Hf[:], PC[:, :, n:n + 1].to_broadcast([128, 8, 64]), ALU.mult, ["Hf", ("PC", n)], ["Hf"])
                ACT(Hb[0:64, :, 0, :], Hf[0:64, :, :], AF.Copy, ["Hf"], ["Hb"])
                ACT(Hb[64:128, :, 1, :], Hf[64:128, :, :], AF.Copy, ["Hf"], ["Hb"])
                for q in range(2):
                    ACT(ybuf[:, q * 512:(q + 1) * 512], PS[by[q]][:, :], AF.Copy, [("ps", by[q])], ["ybuf"])
                y3 = ybuf[:].rearrange("p (a b) -> p a b", a=16)
                t3 = yt1[:].rearrange("p (a b) -> p a b", a=16)
                gs = sb_gn
                RSUM(gs[:, 0:16], y3, ["ybuf"], ["gs"])
                ACT(yt1[:], ybuf[:], AF.Square, ["ybuf"], ["yt1"])
                RSUM(gs[:, 16:32], t3, ["yt1"], ["gs"])
                TS(gs[:, 0:16], gs[:, 0:16], 1.0 / 64.0, None, ALU.mult, None, ["gs"], ["gs"])
                TT(gs[:, 32:48], gs[:, 0:16], gs[:, 0:16], ALU.mult, ["gs"], ["gs"])
                STT(gs[:, 16:32], gs[:, 16:32], 1.0 / 64.0, gs[:, 32:48], ALU.mult, ALU.subtract, ["gs"], ["gs"])
                TS(gs[:, 16:32], gs[:, 16:32], 0.0, 64e-5, ALU.max, ALU.add, ["gs"], ["gs"])
                ACT(gs[:, 16:32], gs[:, 16:32], AF.Sqrt, ["gs"], ["gs"])
                RCP(gs[:, 32:48], gs[:, 16:32], ["gs"], ["gs"])
                TT(y3, y3, gs[:, 0:16].unsqueeze(2).to_broadcast([128, 16, 64]), ALU.subtract, ["ybuf", "gs"], ["ybuf"])
                TT(y3, y3, gs[:, 32:48].unsqueeze(2).to_broadcast([128, 16, 64]), ALU.mult, ["ybuf", "gs"], ["ybuf"])
                TT(ybuf[:], ybuf[:], lnwb[:], ALU.mult, ["ybuf", "lnwb"], ["ybuf"])
                TT(ybuf[:], ybuf[:], lnbb[:], ALU.add, ["ybuf", "lnbb"], ["ybuf"])
                TT(t3, tmV[:].rearrange("p (a b) -> p a b", a=16), rkS[:, n, :].unsqueeze(2).to_broadcast([128, 16, 64]), ALU.mult,
                   ["tmV", "rkS"], ["yt1"])
                TT(ybuf[:], ybuf[:], yt1[:], ALU.add, ["ybuf", "yt1"], ["ybuf"])
                for q in range(2):
                    bk = nb()
                    MM(PS[bk][:, :], sgxa[:, ts], g2a[:, q * 512:(q + 1) * 512], True, False, ["sgxa", "g2a"], [("ps", bk)])
                    MM(PS[bk][:, :], sgxb[:, ts], g2b[:, q * 512:(q + 1) * 512], False, True, ["sgxb", "g2b"], [("ps", bk)])
                    TT(yg[:, q * 512:(q + 1) * 512], ybuf[:, q * 512:(q + 1) * 512], PS[bk][:, :], ALU.mult, ["ybuf", ("ps", bk)], ["yg"])
                bk = nb()
                for c in range(8):
                    TR(PSB[bk][:, c * 128:(c + 1) * 128], yg[:, c * 128:(c + 1) * 128], ["yg"], [("ps", bk)])
                ACT(xnT[:, 0:8, ts], PSB[bk][:, 0:1024].rearrange("p (a b) -> p a b", a=8), AF.Copy, [("ps", bk)], [("xnT", c) for c in range(8)])

            if lim < 4.4:
                return
            for b in range(8):
                s = nwB()
                WL(wB[s], [("wB", s, 0), ("wB", s, 1)], ("wB", s), [(wB[s][:], woutv[:, cs(2 * b, 2 * b + 2), :])])
                for j in range(2):
                    kc = 2 * b + j
                    for tb in range(NTB):
                        for db in range(4):
                            MM(PS[tb * 4 + db][:, :], xnT[:, kc, tb * 128:(tb + 1) * 128], wB[s][:, j, db * 512:(db + 1) * 512],
                               kc == 0, kc == 15, [("xnT", kc), ("wB", s, j)], [("ps", tb * 4 + db)])
            for tb in range(NTB):
                for db in range(4):
                    hs = h[:, tb, db * 512:(db + 1) * 512]
                    TT(hs, hs, PS[tb * 4 + db][:, :], ALU.add, [("ps", tb * 4 + db), ("h", tb)], [("h", tb)])

        sb_gn = sb("gs", [128, 48], F32)
        finals = []
        for ti in range(NT):
            cur["ti"] = ti
            cur["blk"] = 0
            for tb in range(NTB):
                DMA("sp", h[:, tb, :], xv[ti * NTB + tb], [], [("h", tb)], ("hld", tb))
            if lim >= 1:
                norm_stage(G1)
            if lim >= 2:
                ffn_stage(0)
            if lim >= 3:
                norm_stage(GM)
            if lim >= 4:
                mixer_stage(lim)
            if lim >= 5:
                norm_stage(G2)
                ffn_stage(1)
            for tb in range(NTB):
                ss = stat[:, 48:49]
                ACT(xnb[:], h[:, tb, :], AF.Square, [("h", tb)], ["xnb", "stat"], accum=ss)
                rs = rstd_of(ss, 1, 1.0 / D, 1e-6)
                STT(h[:, tb, :], h[:, tb, :], rs, fnb[:], ALU.mult, ALU.mult, [("h", tb), "stat", "fnb"], [("h", tb)])
                finals.append(DMA("sp", ov[ti * NTB + tb], h[:, tb, :], [("h", tb)], [], ("hst", tb)))
        P.emit(st, finals + st_tokens)
    return nc


def _pack(inputs):
    f = lambda k: np.asarray(inputs[k], dtype=np.float32)
    sq = lambda k: f(k)[0]
    colp = np.zeros((128, NCOL_IN), np.float32)
    pc = lambda v: np.ascontiguousarray(v.reshape(-1, 128).T)
    colp[:, G1:G1 + 16] = pc(sq("ffn1_norm"))
    colp[:, GM:GM + 16] = pc(sq("mix_norm"))
    colp[:, G2:G2 + 16] = pc(sq("ffn2_norm"))
    mu = sq("rwkv_mu")
    colp[:, MU:MU + 8] = pc(mu[0:1024])
    colp[:, MU + 8:MU + 16] = pc(mu[1088:2112])
    colp[:, MU + 16:MU + 24] = pc(mu[2112:3136])
    colp[0:64, MU + 24] = mu[1024:1088]
    colp[64:128, MU + 24] = mu[3136:3200]
    colp[:, MU + 25] = mu[3200:3328]
    colp[0:32, MU + 26] = mu[3328:3360]
    colp[:, W0:W0 + 8] = pc(sq("rwkv_w0"))
    colp[:, A0:A0 + 8] = pc(sq("rwkv_a0"))
    colp[:, KK:KK + 8] = pc(sq("rwkv_k_k"))
    colp[:, KA:KA + 8] = pc(sq("rwkv_k_a"))
    colp[:, RK:RK + 8] = pc(sq("rwkv_r_k").reshape(-1))
    cw = sq("lru_conv_w")
    for j in range(4):
        colp[:, CW + j * 8:CW + j * 8 + 8] = pc(cw[j])
    colp[:, CB:CB + 8] = pc(sq("lru_conv_b"))
    colp[:, BA:BA + 8] = pc(sq("lru_ba"))
    colp[:, BX:BX + 8] = pc(sq("lru_bx"))
    colp[:, LAM:LAM + 8] = pc(sq("lru_lam"))
    colp[:, LN:LN + 8] = pc(sq("lru_norm"))
    w2p = np.zeros((128, 1024), np.float32)
    w2p[0:64] = sq("rwkv_w2")
    a2p = np.zeros((128, 1024), np.float32)
    a2p[64:128] = sq("rwkv_a2")

    def bd(w):
        o = np.zeros((128, 8, 128), np.float32)
        for c in range(8):
            o[0:64, c, 0:64] = w[2 * c]
            o[64:128, c, 64:128] = w[2 * c + 1]
        return o

    shared = dict(
        wg1=sq("ffn1_w_gate"), wu1=sq("ffn1_w_up"), wd1=sq("ffn1_w_down"),
        wg2=sq("ffn2_w_gate"), wu2=sq("ffn2_w_up"), wd2=sq("ffn2_w_down"),
        w_in=sq("w_in"), w_out=sq("w_out"), colp=colp, w2p=w2p, a2p=a2p,
        g2=sq("rwkv_g2"), wabd=bd(sq("lru_wa")), wxbd=bd(sq("lru_wx")),
        lnwb=np.ascontiguousarray(np.broadcast_to(sq("rwkv_ln_w")[None, :], (128, 1024))),
        lnbb=np.ascontiguousarray(np.broadcast_to(sq("rwkv_ln_b")[None, :], (128, 1024))),
        fnb=np.ascontiguousarray(np.broadcast_to(f("final_norm")[None, :], (128, D))),
    )
    return shared


_NT = SEQ // T


def kernel(**inputs):
    x = np.asarray(inputs["x"], dtype=np.float32)
    shared = _pack(inputs)
    nc = build(_NT)
    in_maps = [dict(shared, x=np.ascontiguousarray(x[b])) for b in range(N_CORES)]
    res = run_bass_kernel_spmd(nc, in_maps, core_ids=list(range(N_CORES)))
    return np.stack([res.results[b]["out"] for b in range(N_CORES)], axis=0)
```

```python
import contextlib
import numpy as np
import concourse.bass as bass
import concourse.mybir as mybir
from concourse.bass_utils import run_bass_kernel_spmd

F32 = mybir.dt.float32
BF16 = mybir.dt.bfloat16
AF = mybir.ActivationFunctionType
ALU = mybir.AluOpType
AX = mybir.AxisListType

D = 2048
DFF = 5632
SEQ = 8192
T = 256
NTB = 2
DIN = 5408
N_CORES = 4
CD = 0.5 * float(np.exp(-0.5))

G1, GM, G2, MU, W0, A0, KK, KA, RK, CW, CB, BA, BX, LAM, LN = 0, 16, 32, 48, 75, 83, 91, 99, 107, 115, 147, 155, 163, 171, 179
NCOL_IN = 187
OMM, HW0, HA0, C1, C0, HBA, HBX, CA, CA2 = 187, 214, 222, 230, 238, 246, 254, 262, 270
NCOL = 288

ENGS = ("pe", "act", "dve", "pool", "sp")


class Prog:
    def __init__(self, nc):
        self.nc = nc
        self.ops = {e: [] for e in ENGS}
        self.last_w = {}
        self.readers = {}
        self.dma_cnt = {}

    def op(self, eng, fn, reads=(), writes=(), dma=None, after=()):
        deps = []
        for w in after:
            t = self.last_w.get(w)
            if t is not None:
                deps.append(t)
            deps.extend(self.readers.get(w, ()))
        for r in reads:
            t = self.last_w.get(r)
            if t is not None:
                deps.append(t)
        for w in writes:
            t = self.last_w.get(w)
            if t is not None:
                deps.append(t)
            deps.extend(self.readers.get(w, ()))
        idx = len(self.ops[eng])
        if dma is not None:
            n = self.dma_cnt.get(dma, 0) + 1
            self.dma_cnt[dma] = n
            tok = ("d", dma, n)
        else:
            tok = ("c", eng, idx)
        best = {}
        for t in deps:
            if t[0] == "c" and t[1] == eng and eng == "pe":
                continue
            k = (t[0], t[1])
            if k not in best or t[2] > best[k][2]:
                best[k] = t
        dd = list(best.values())
        self.ops[eng].append(dict(fn=fn, deps=dd, inc=False, dma=dma))
        for t in dd:
            if t[0] == "c":
                self.ops[t[1]][t[2]]["inc"] = True
        for r in reads:
            self.readers.setdefault(r, []).append(tok)
        for w in writes:
            self.last_w[w] = tok
            self.readers[w] = []
        return tok

    def emit(self, st, final_tokens):
        nc = self.nc
        cum = {}
        for e in ENGS:
            c = 0
            arr = []
            for o in self.ops[e]:
                if o["inc"] and o["dma"] is None:
                    c += 1
                arr.append(c)
            cum[e] = arr
        esem = {e: st.enter_context(nc.semaphore("s_" + e)) for e in ENGS if e != "sp"}
        dsem = {k: st.enter_context(nc.semaphore("d_%d" % i)) for i, k in enumerate(self.dma_cnt)}
        block = st.enter_context(nc.Block())

        def cond(t):
            if t[0] == "c":
                return esem[t[1]], cum[t[1]][t[2]], ("c", t[1])
            return dsem[t[1]], 16 * t[2], ("d", t[1])

        def run(e, engobj, extra=None):
            waited = {}

            def w(t):
                s, v, k = cond(t)
                if waited.get(k, 0) >= v:
                    return
                engobj.wait_ge(s, v)
                waited[k] = v

            for o in self.ops[e]:
                for t in o["deps"]:
                    w(t)
                ins = o["fn"](engobj)
                if o["dma"] is not None:
                    ins.then_inc(dsem[o["dma"]], 16)
                elif o["inc"]:
                    ins.then_inc(esem[e], 1)
            for t in (extra or ()):
                w(t)

        @block.tensor
        def _(eng):
            run("pe", eng)

        @block.scalar
        def _(eng):
            run("act", eng)

        @block.vector
        def _(eng):
            run("dve", eng)

        @block.gpsimd
        def _(eng):
            run("pool", eng)

        @block.sync
        def _(eng):
            run("sp", eng, final_tokens)


def build(NT, dbg=False, lim=99, tiny=False):
    nc = bass.Bass("TRN2", target_bir_lowering=False)
    di = lambda name, shape: nc.dram_tensor(name, shape, F32, kind="ExternalInput").ap()
    seq = NT * T if tiny else SEQ
    dff_, din_, dm_ = (256, 256, 256) if tiny else (DFF, DIN, D)
    cs = (lambda a, b: slice(0, b - a)) if tiny else (lambda a, b: slice(a, b))
    x_d = di("x", [seq, D])
    wg_d = [di("wg1", [D, dff_]), di("wg2", [D, dff_])]
    wu_d = [di("wu1", [D, dff_]), di("wu2", [D, dff_])]
    wd_d = [di("wd1", [dff_, D]), di("wd2", [dff_, D])]
    win_d = di("w_in", [D, din_])
    wout_d = di("w_out", [dm_, D])
    colp_d = di("colp", [128, NCOL_IN])
    w2p_d = di("w2p", [128, 1024])
    a2p_d = di("a2p", [128, 1024])
    g2_d = di("g2", [160, 1024])
    wabd_d = di("wabd", [128, 8, 128])
    wxbd_d = di("wxbd", [128, 8, 128])
    lnw_d = di("lnwb", [128, 1024])
    lnb_d = di("lnbb", [128, 1024])
    fn_d = di("fnb", [128, D])
    out_d = nc.dram_tensor("out", [seq, D], F32, kind="ExternalOutput").ap()

    st = contextlib.ExitStack()
    with st:
        sb = lambda name, shape, dt: st.enter_context(nc.sbuf_tensor("sb_" + name, shape, dt))
        h = sb("h", [128, NTB, D], F32)
        xnT = sb("xnT", [128, 16, T], BF16)
        act = sb("act", [128, 44, T], BF16)
        wA = [sb("wA%d" % i, [128, 16, 256], BF16) for i in range(3)]
        wB = [sb("wB%d" % i, [128, 2, D], BF16) for i in range(2)]
        colp = sb("colp", [128, NCOL], F32)
        w2p = sb("w2ps", [128, 1024], BF16)
        a2p = sb("a2ps", [128, 1024], BF16)
        g2a = sb("g2a", [128, 1024], BF16)
        g2b = sb("g2b", [128, 1024], BF16)
        wabd = sb("wabds", [128, 8, 128], BF16)
        wxbd = sb("wxbds", [128, 8, 128], BF16)
        lnwb = sb("lnwbs", [128, 1024], F32)
        lnbb = sb("lnbbs", [128, 1024], F32)
        fnb = sb("fnbs", [128, D], F32)
        ident = sb("ident", [128, 128], BF16)
        bones = sb("bones", [128, 128], BF16)
        ones = sb("ones", [128, 128], BF16)
        hind = sb("hind", [128, 2], BF16)
        mU = sb("mU", [128, 4, 128], BF16)
        mUi = sb("mUi", [128, 4, 128], BF16)
        mL = sb("mL", [128, 4, 128], BF16)
        rmask = sb("rmask", [128, T], F32)
        xnb = sb("xnb", [128, D], BF16)
        stat = sb("stat", [128, 64], F32)
        carry = sb("carry", [128, 32], F32)
        tsh = [sb("tsh%d" % i, [128, 4 + T], F32) for i in range(2)]
        lorab = sb("lorab", [128, T], BF16)
        sgxa = sb("sgxa", [128, T], BF16)
        sgxb = sb("sgxb", [128, T], BF16)
        xbuf = sb("xbuf", [128, 8, 4 + T], BF16)
        ylb = sb("ylb", [128, 8, T], BF16)
        hcar = sb("hcar", [128, 8], F32)
        vT = sb("vT", [128, 8, T], BF16)
        PC = sb("PC", [128, 8, NTB], F32)
        rkS = sb("rkS", [128, NTB, 16], F32)
        lrs = sb("lrs", [128, T], F32)
        tmA = sb("tmA", [128, 1024], BF16)
        tmBk = sb("tmBk", [128, 1024], BF16)
        tmK = sb("tmK", [128, 1024], BF16)
        tmV = sb("tmV", [128, 1024], BF16)
        AkT = sb("AkT", [128, 16, 128], BF16)
        ArbT = sb("ArbT", [128, 16, 128], BF16)
        ArkT = sb("ArkT", [128, 16, 128], BF16)
        Pm = [sb("Pm%d" % i, [128, 4, 128], BF16) for i in range(2)]
        Qm = [sb("Qm%d" % i, [128, 4, 128], BF16) for i in range(2)]
        Xf = sb("Xf", [128, 4, 128], F32)
        Xb = sb("Xb", [128, 4, 128], BF16)
        AwT = sb("AwT", [128, 8, 128], BF16)
        Uv = sb("Uv", [128, 1024], BF16)
        Ub = sb("Ub", [128, 1024], BF16)
        Hf = sb("Hf", [128, 8, 64], F32)
        Hb = sb("Hb", [128, 8, 2, 64], BF16)
        ybuf = sb("ybuf", [128, 1024], F32)
        yt1 = sb("yt1", [128, 1024], BF16)
        yg = sb("yg", [128, 1024], BF16)
        NTF, NTBF = 18, 6
        tpf_all = sb("tpf", [128, NTF, T], F32)
        tpf = [tpf_all[:, i, :] for i in range(NTF)]
        vbf = lambda i: tpf_all[:, i, :].bitcast(BF16).rearrange("p (a b) -> p a b", a=4)
        SETS = [
            dict(Pm=[Pm[0][:], Pm[1][:]], Qm=[Qm[0][:], Qm[1][:]], Xf=Xf[:], Xb=Xb[:],
                 kP=[[("Pm", 0)], [("Pm", 1)]], kQ=[[("Qm", 0)], [("Qm", 1)]], kXf=["Xf"], kXb=["Xb"]),
            dict(Pm=[vbf(0), vbf(1)], Qm=[vbf(2), vbf(3)], Xb=vbf(4),
                 Xf=tpf_all[:, 5:7, :].rearrange("p a (b c) -> p (a b) c", c=128),
                 kP=[[("tpf", 0)], [("tpf", 1)]], kQ=[[("tpf", 2)], [("tpf", 3)]], kXf=[("tpf", 5), ("tpf", 6)], kXb=[("tpf", 4)]),
        ]
        tpb = [sb("tpb%d" % i, [128, T], BF16) for i in range(NTBF)]
        PS = [st.enter_context(nc.psum_tensor("ps%d" % i, [128, 512], F32)) for i in range(8)]
        PSB = [p[:].bitcast(BF16) for p in PS]

        P = Prog(nc)
        cnt = dict(f=0, b=0, bank=0, wA=0, wB=0, tsh=0, res=set())

        def tmpf():
            i = cnt["f"] % NTF
            cnt["f"] += 1
            return tpf[i], ("tpf", i)

        def tmpb():
            i = cnt["b"] % NTBF
            cnt["b"] += 1
            return tpb[i], ("tpb", i)

        def nb(reserve=False):
            while True:
                i = cnt["bank"] % 8
                cnt["bank"] += 1
                if i not in cnt["res"]:
                    break
            if reserve:
                cnt["res"].add(i)
            return i

        def nwA():
            i = cnt["wA"] % 3
            cnt["wA"] += 1
            return i

        def nwB():
            i = cnt["wB"] % 2
            cnt["wB"] += 1
            return i

        def MM(out, lhsT, rhs, st_, sp_, R, W):
            P.op("pe", lambda e: e.matmul(out, lhsT=lhsT, rhs=rhs, start=st_, stop=sp_), R, W)

        def TR(out, in_, R, W):
            K = in_.shape[0]
            P.op("pe", lambda e: e.transpose(out=out, in_=in_, identity=ident[0:K, 0:K]), list(R) + ["ident"], W)

        def ACT(out, in_, func, R, W, scale=1.0, bias=None, accum=None):
            kw = {}
            if bias is not None:
                kw["bias"] = bias
            if accum is not None:
                kw["accum_out"] = accum
            P.op("act", lambda e: e.activation(out=out, in_=in_, func=func, scale=scale, **kw), R, W)

        def TT(out, in0, in1, op, R, W, eng="dve"):
            P.op(eng, lambda e: e.tensor_tensor(out=out, in0=in0, in1=in1, op=op), R, W)

        def TS(out, in0, s1, s2, op0, op1, R, W, eng="dve"):
            if s2 is None:
                P.op(eng, lambda e: e.tensor_scalar(out=out, in0=in0, scalar1=s1, scalar2=None, op0=op0), R, W)
            else:
                P.op(eng, lambda e: e.tensor_scalar(out=out, in0=in0, scalar1=s1, scalar2=s2, op0=op0, op1=op1), R, W)

        def STT(out, in0, sc, in1, op0, op1, R, W):
            P.op("dve", lambda e: e.scalar_tensor_tensor(out=out, in0=in0, scalar=sc, in1=in1, op0=op0, op1=op1), R, W)

        def CP(out, in_, R, W, eng="dve"):
            P.op(eng, lambda e: e.tensor_copy(out=out, in_=in_), R, W)

        def RCP(out, in_, R, W):
            P.op("dve", lambda e: e.reciprocal(out=out, in_=in_), R, W)

        def SCAN(out, d0, d1, init, R, W):
            P.op("dve", lambda e: e.tensor_tensor_scan(out=out, data0=d0, data1=d1, initial=init, op0=ALU.mult, op1=ALU.add), R, W)

        def RSUM(out, in_, R, W):
            P.op("dve", lambda e: e.tensor_reduce(out=out, in_=in_, axis=AX.X, op=ALU.add), R, W)

        def DMA(eng, out, in_, R, W, key, after=()):
            return P.op(eng, lambda e: e.dma_start(out=out, in_=in_), R, W, dma=key, after=after)

        NBLK = 180
        scr = nc.dram_tensor("wscr", [NBLK, 128, 4096], BF16, kind="Internal").ap()
        cur = {"ti": 0, "blk": 0}
        st_tokens = []

        def WL(buf, wkeys, key, pieces):
            b = cur["blk"]
            cur["blk"] += 1
            assert b < NBLK
            flat = buf[:].rearrange("p a b -> p (a b)")
            if cur["ti"] == 0:
                for i, (dst, src) in enumerate(pieces):
                    if i == len(pieces) - 1:
                        DMA("pool", dst, src, [], wkeys, key)
                    else:
                        DMA("pool", dst, src, [], [], key, after=wkeys)
                st_tokens.append(DMA("sp", scr[b], flat, wkeys, [("scr", b)], ("scrst", key)))
            else:
                DMA("sp", flat, scr[b], [("scr", b)], wkeys, ("hw", key))

        def run2(gens):
            active = list(gens)
            while active:
                for g in list(active):
                    try:
                        next(g)
                    except StopIteration:
                        active.remove(g)

        def MSET(ap, val, W):
            P.op("pool", lambda e: e.memset(ap, val), [], W)

        def ASEL(ap, pattern, cmp, base, cm, W):
            P.op("pool", lambda e: e.affine_select(out=ap, in_=ap, pattern=pattern, compare_op=cmp, fill=0.0, base=base, channel_multiplier=cm), W, W)

        col = lambda c0, n=1: colp[:, c0:c0 + n]

        DMA("sp", colp[:, 0:NCOL_IN], colp_d, [], ["colp"], "colp")
        DMA("sp", lnwb[:], lnw_d, [], ["lnwb"], "lnwb")
        DMA("sp", lnbb[:], lnb_d, [], ["lnbb"], "lnbb")
        DMA("sp", fnb[:], fn_d, [], ["fnb"], "fnb")
        DMA("pool", w2p[:], w2p_d, [], ["w2p"], "w2p")
        DMA("pool", a2p[:], a2p_d, [], ["a2p"], "a2p")
        DMA("pool", g2a[:], g2_d[0:128, :], [], ["g2a"], "g2a")
        MSET(g2b[:], 0.0, ["g2b"])
        MSET(sgxb[:], 0.0, ["sgxb"])
        DMA("pool", g2b[0:32, :], g2_d[128:160, :], [], ["g2b"], "g2b")
        DMA("pool", wabd[:], wabd_d, [], ["wabd"], "wabd")
        DMA("pool", wxbd[:], wxbd_d, [], ["wxbd"], "wxbd")
        MSET(ident[:], 1.0, ["ident"])
        ASEL(ident[:], [[-1, 128]], ALU.is_equal, 0, 1, ["ident"])
        MSET(ones[:], 1.0, ["ones"])
        MSET(bones[:], 1.0, ["bones"])
        MSET(bones[0:64, 64:128], 0.0, ["bones"])
        MSET(bones[64:128, 0:64], 0.0, ["bones"])
        MSET(hind[:], 0.0, ["hind"])
        MSET(hind[0:64, 0:1], 1.0, ["hind"])
        MSET(hind[64:128, 1:2], 1.0, ["hind"])
        MSET(mU[:], 1.0, ["mU"])
        ASEL(mU[:], [[0, 4], [1, 128]], ALU.is_gt, 0, -1, ["mU"])
        MSET(mUi[:], 1.0, ["mUi"])
        ASEL(mUi[:], [[0, 4], [1, 128]], ALU.is_ge, 0, -1, ["mUi"])
        MSET(mL[:], 1.0, ["mL"])
        ASEL(mL[:], [[0, 4], [-1, 128]], ALU.is_gt, 0, 1, ["mL"])
        MSET(rmask[:], 1.0, ["rmask"])
        MSET(rmask[:, 0:1], 0.0, ["rmask"])
        MSET(rmask[:, 128:129], 0.0, ["rmask"])
        MSET(carry[:], 0.0, ["carry"])
        MSET(xbuf[:], 0.0, ["xbuf"])
        MSET(hcar[:], 0.0, ["hcar"])
        MSET(Hf[:], 0.0, ["Hf"])
        MSET(Hb[:], 0.0, ["Hb"])
        MSET(stat[:], 0.0, ["stat"])
        TS(col(OMM, 27), col(MU, 27), -1.0, 1.0, ALU.mult, ALU.add, ["colp"], ["colp"])
        TS(col(HW0, 8), col(W0, 8), 0.5, None, ALU.mult, None, ["colp"], ["colp"])
        TS(col(HA0, 8), col(A0, 8), 0.5, None, ALU.mult, None, ["colp"], ["colp"])
        TS(col(C1, 8), col(KA, 8), 0.5, None, ALU.mult, None, ["colp"], ["colp"])
        TS(col(C0, 8), col(KA, 8), -0.5, 1.0, ALU.mult, ALU.add, ["colp"], ["colp"])
        TS(col(HBA, 8), col(BA, 8), 0.5, None, ALU.mult, None, ["colp"], ["colp"])
        TS(col(HBX, 8), col(BX, 8), 0.5, None, ALU.mult, None, ["colp"], ["colp"])
        ev = stat[:, 32:40]
        pv = stat[:, 40:48]
        ACT(ev, col(LAM, 8), AF.Exp, ["colp"], ["stat"], scale=-1.0)
        TS(pv, ev, 0.2, -0.25, ALU.mult, ALU.add, ["stat"], ["stat"])
        TT(pv, pv, ev, ALU.mult, ["stat"], ["stat"])
        TS(pv, pv, 1.0 / 3.0, None, ALU.add, None, ["stat"], ["stat"])
        TT(pv, pv, ev, ALU.mult, ["stat"], ["stat"])
        TS(pv, pv, -0.5, None, ALU.add, None, ["stat"], ["stat"])
        TT(pv, pv, ev, ALU.mult, ["stat"], ["stat"])
        TS(pv, pv, 1.0, None, ALU.add, None, ["stat"], ["stat"])
        TT(pv, pv, ev, ALU.mult, ["stat"], ["stat"])
        TS(col(CA, 8), pv, -4.0, None, ALU.mult, None, ["stat"], ["colp"])
        TS(col(CA2, 8), pv, -8.0, None, ALU.mult, None, ["stat"], ["colp"])

        wgv = [w.rearrange("(kc p) n -> p kc n", p=128) for w in wg_d]
        wuv = [w.rearrange("(kc p) n -> p kc n", p=128) for w in wu_d]
        wdv = [w.rearrange("(hc p) n -> p hc n", p=128) for w in wd_d]
        winv = win_d.rearrange("(kc p) n -> p kc n", p=128)
        woutv = wout_d.rearrange("(kc p) n -> p kc n", p=128)
        xv = x_d.rearrange("(n p) d -> n p d", p=128)
        ov = out_d.rearrange("(n p) d -> n p d", p=128)

        def rstd_of(ss_ap, n, inv_n, eps, key="stat"):
            ms = stat[:, 16:16 + n]
            TS(ms, ss_ap, inv_n, eps, ALU.mult, ALU.add, [key], ["stat"])
            ACT(ms, ms, AF.Sqrt, ["stat"], ["stat"])
            rs = stat[:, 0:n]
            RCP(rs, ms, ["stat"], ["stat"])
            return rs

        def norm_stage(gcol):
            for tb in range(NTB):
                ss = stat[:, 48:49]
                ACT(xnb[:], h[:, tb, :], AF.Square, [("h", tb)], ["xnb", "stat"], accum=ss)
                rs = rstd_of(ss, 1, 1.0 / D, 1e-6)
                ACT(xnb[:], h[:, tb, :], AF.Identity, [("h", tb), "stat"], ["xnb"], scale=rs)
                for half in range(2):
                    bk = nb()
                    for j in range(8):
                        kc = half * 8 + j
                        TR(PSB[bk][:, j * 128:(j + 1) * 128], xnb[:, kc * 128:(kc + 1) * 128], ["xnb"], [("ps", bk)])
                    TT(xnT[:, half * 8:(half + 1) * 8, tb * 128:(tb + 1) * 128],
                       PSB[bk][:, 0:1024].rearrange("p (a b) -> p a b", a=8),
                       colp[:, gcol + half * 8:gcol + half * 8 + 8].unsqueeze(2).to_broadcast([128, 8, 128]),
                       ALU.mult, [("ps", bk), "colp"], [("xnT", kc) for kc in range(half * 8, half * 8 + 8)])

        def ffn_stage(l):
            for b in range(22):
                sg_, su_ = nwA(), nwA()
                WL(wA[sg_], [("wA", sg_, 0), ("wA", sg_, 1)], ("wA", sg_), [(wA[sg_][:], wgv[l][:, :, cs(b * 256, (b + 1) * 256)])])
                WL(wA[su_], [("wA", su_, 0), ("wA", su_, 1)], ("wA", su_), [(wA[su_][:], wuv[l][:, :, cs(b * 256, (b + 1) * 256)])])
                for j in range(2):
                    hc = 2 * b + j
                    bk = nb()
                    for kc in range(16):
                        MM(PS[bk][:, 0:T], wA[sg_][:, kc, j * 128:(j + 1) * 128], xnT[:, kc, :], kc == 0, kc == 15,
                           [("wA", sg_, j), ("xnT", kc)], [("ps", bk)])
                    for kc in range(16):
                        MM(PS[bk][:, 256:256 + T], wA[su_][:, kc, j * 128:(j + 1) * 128], xnT[:, kc, :], kc == 0, kc == 15,
                           [("wA", su_, j), ("xnT", kc)], [("ps", bk)])
                    th, kth = tmpf()
                    ACT(th[:], PS[bk][:, 0:T], AF.Tanh, [("ps", bk)], [kth], scale=0.5)
                    t1, kt1 = tmpf()
                    STT(t1[:], th[:], 1.0, PS[bk][:, 0:T], ALU.add, ALU.mult, [kth, ("ps", bk)], [kt1])
                    STT(act[:, hc, :], t1[:], 0.5, PS[bk][:, 256:256 + T], ALU.mult, ALU.mult, [kt1, ("ps", bk)], [("act", hc)])
            for b in range(22):
                s = nwB()
                WL(wB[s], [("wB", s, 0), ("wB", s, 1)], ("wB", s), [(wB[s][:], wdv[l][:, cs(2 * b, 2 * b + 2), :])])
                for j in range(2):
                    hc = 2 * b + j
                    for tb in range(NTB):
                        for db in range(4):
                            MM(PS[tb * 4 + db][:, :], act[:, hc, tb * 128:(tb + 1) * 128], wB[s][:, j, db * 512:(db + 1) * 512],
                               hc == 0, hc == 43, [("act", hc), ("wB", s, j)], [("ps", tb * 4 + db)])
            for tb in range(NTB):
                for db in range(4):
                    hs = h[:, tb, db * 512:(db + 1) * 512]
                    STT(hs, PS[tb * 4 + db][:, :], 0.5, hs, ALU.mult, ALU.add, [("ps", tb * 4 + db), ("h", tb)], [("h", tb)])

        def proj(slot, half, M):
            bk = nb()
            for kc in range(16):
                MM(PS[bk][0:M, 0:T], wA[slot][:, kc, half * 128:half * 128 + M], xnT[:, kc, :], kc == 0, kc == 15,
                   [("wA", slot, half), ("xnT", kc)], [("ps", bk)])
            return bk

        def shift_evac(oc, bk, M, out, okey):
            i = cnt["tsh"] % 2
            cnt["tsh"] += 1
            ts_ = tsh[i]
            k = ("tsh", i)
            ACT(ts_[0:M, 3:4], carry[0:M, oc:oc + 1], AF.Copy, ["carry"], [k])
            ACT(ts_[0:M, 4:4 + T], PS[bk][0:M, 0:T], AF.Identity, [("ps", bk), "colp"], [k], scale=colp[0:M, MU + oc:MU + oc + 1])
            ACT(carry[0:M, oc:oc + 1], ts_[0:M, 3 + T:4 + T], AF.Copy, [k], ["carry"])
            STT(out, PS[bk][0:M, 0:T], colp[0:M, OMM + oc:OMM + oc + 1], ts_[0:M, 3:3 + T], ALU.mult, ALU.add,
                [("ps", bk), k, "colp"], [okey])

        def mixer_stage(lim=99):
            s0 = nwA()
            WL(wA[s0], [("wA", s0, 0), ("wA", s0, 1)], ("wA", s0),
               [(wA[s0][:, :, 0:64], winv[:, :, cs(1024, 1088)]), (wA[s0][:, :, 64:128], winv[:, :, cs(3136, 3200)]),
                (wA[s0][:, :, 128:256], winv[:, :, cs(3200, 3328)])])
            s1 = nwA()
            WL(wA[s1], [("wA", s1, 0), ("wA", s1, 1)], ("wA", s1), [(wA[s1][:, :, 0:32], winv[:, :, cs(3328, 3360)])])
            bk = proj(s0, 0, 128)
            t0, k0 = tmpf()
            shift_evac(24, bk, 128, t0[:], k0)
            ACT(lorab[0:64, :], t0[0:64, :], AF.Tanh, [k0], ["lorab"])
            ACT(lorab[64:128, :], t0[64:128, :], AF.Copy, [k0], ["lorab"])
            bk = proj(s0, 1, 128)
            t0, k0 = tmpf()
            shift_evac(25, bk, 128, t0[:], k0)
            t1, k1 = tmpf()
            ACT(t1[:], t0[:], AF.Tanh, [k0], [k1], scale=0.5)
            TS(sgxa[:], t1[:], 0.5, 0.5, ALU.mult, ALU.add, [k1], ["sgxa"])
            bk = proj(s1, 0, 32)
            t0, k0 = tmpf()
            shift_evac(26, bk, 32, t0[0:32, :], k0)
            t1, k1 = tmpf()
            ACT(t1[0:32, :], t0[0:32, :], AF.Tanh, [k0], [k1], scale=0.5)
            TS(sgxb[0:32, :], t1[0:32, :], 0.5, 0.5, ALU.mult, ALU.add, [k1], ["sgxb"])

            if lim < 4.1:
                return
            ACT(xbuf[:, :, 1:4], xbuf[:, :, 1 + T:4 + T], AF.Copy, ["xbuf"] + [("xb", c) for c in range(8)], ["xbuf"])
            bsum = nb(True)
            def lru_chunk(c, sx, sgt):
                bk = proj(sx, c % 2, 128)
                yield
                ACT(xbuf[:, c, 4:4 + T], PS[bk][:, 0:T], AF.Copy, [("ps", bk), "xbuf"], [("xb", c)])
                yield
                bk = proj(sgt, c % 2, 128)
                yield
                gt = act[:, 32 + c, :]
                ACT(gt, PS[bk][:, 0:T], AF.Copy, [("ps", bk)], [("act", 32 + c)])
                yield
                if lim < 4.11:
                    return
                xc, kxc = tmpf()
                TS(xc[:], xbuf[:, c, 1:1 + T], col(CW + c), col(CB + c), ALU.mult, ALU.add, [("xb", c), "xbuf", "colp"], [kxc])
                yield
                for jj in (1, 2, 3):
                    STT(xc[:], xbuf[:, c, 1 + jj:1 + jj + T], col(CW + jj * 8 + c), xc[:], ALU.mult, ALU.add, [("xb", c), "xbuf", kxc, "colp"], [kxc])
                    yield
                xcb, kxcb = tmpb()
                ACT(xcb[:], xc[:], AF.Copy, [kxc], [kxcb])
                yield
                bk = nb()
                MM(PS[bk][:, 0:T], wabd[:, c, :], xcb[:], True, True, ["wabd", kxcb], [("ps", bk)])
                yield
                MM(PS[bk][:, 256:256 + T], wxbd[:, c, :], xcb[:], True, True, ["wxbd", kxcb], [("ps", bk)])
                yield
                thr, kthr = tmpf()
                ACT(thr[:], PS[bk][:, 0:T], AF.Tanh, [("ps", bk), "colp"], [kthr], scale=0.5, bias=col(HBA + c))
                yield
                thi, kthi = tmpf()
                ACT(thi[:], PS[bk][:, 256:256 + T], AF.Tanh, [("ps", bk), "colp"], [kthi], scale=0.5, bias=col(HBX + c))
                yield
                if lim < 4.12:
                    return
                a_, ka_ = tmpf()
                ACT(a_[:], thr[:], AF.Exp, [kthr, "colp"], [ka_], scale=col(CA + c), bias=col(CA + c))
                yield
                a2, ka2 = tmpf()
                ACT(a2[:], thr[:], AF.Exp, [kthr, "colp"], [ka2], scale=col(CA2 + c), bias=col(CA2 + c))
                yield
                TS(a2[:], a2[:], -1.0, 1.0, ALU.mult, ALU.add, [ka2], [ka2])
                yield
                TS(a2[:], a2[:], 0.0, None, ALU.max, None, [ka2], [ka2])
                yield
                ACT(a2[:], a2[:], AF.Sqrt, [ka2], [ka2])
                yield
                ui, kui = tmpf()
                STT(ui[:], thi[:], 1.0, xc[:], ALU.add, ALU.mult, [kthi, kxc], [kui])
                yield
                STT(ui[:], ui[:], 0.5, a2[:], ALU.mult, ALU.mult, [kui, ka2], [kui])
                yield
                if lim < 4.13:
                    return
                hs, khs = tmpf()
                SCAN(hs[:], a_[:], ui[:], hcar[:, c:c + 1], [ka_, kui, "hcar"], [khs])
                yield
                CP(hcar[:, c:c + 1], hs[:, T - 1:T], [khs], ["hcar"])
                yield
                if lim < 4.14:
                    return
                x2, kx2 = tmpf()
                ACT(x2[:], gt, AF.Square, [("act", 32 + c)], [kx2])
                yield
                TS(x2[:], x2[:], 0.044715, 1.0, ALU.mult, ALU.add, [kx2], [kx2])
                yield
                TT(x2[:], x2[:], gt, ALU.mult, [kx2, ("act", 32 + c)], [kx2])
                yield
                ACT(x2[:], x2[:], AF.Tanh, [kx2], [kx2], scale=0.7978845608028654)
                yield
                STT(x2[:], x2[:], 1.0, gt, ALU.add, ALU.mult, [kx2, ("act", 32 + c)], [kx2])
                yield
                STT(ylb[:, c, :], x2[:], 0.5, hs[:], ALU.mult, ALU.mult, [kx2, khs], [("ylb", c)])
                yield
                if lim < 4.15:
                    return
                sq, ksq = tmpb()
                ACT(sq[:], ylb[:, c, :], AF.Square, [("ylb", c)], [ksq])
                yield
                MM(PS[bsum][:, 0:T], ones[:], sq[:], c == 0, c == 7, ["ones", ksq], [("ps", bsum)])
                yield
            for cp in range(4):
                c = 2 * cp
                sx = nwA()
                WL(wA[sx], [("wA", sx, 0), ("wA", sx, 1)], ("wA", sx), [(wA[sx][:], winv[:, :, cs(3360 + c * 128, 3360 + c * 128 + 256)])])
                sgt = nwA()
                WL(wA[sgt], [("wA", sgt, 0), ("wA", sgt, 1)], ("wA", sgt), [(wA[sgt][:], winv[:, :, cs(4384 + c * 128, 4384 + c * 128 + 256)])])
                run2([lru_chunk(c, sx, sgt), lru_chunk(c + 1, sx, sgt)])
            cnt["res"].discard(bsum)
            if lim < 4.16:
                return
            rs_, krs = tmpf()
            TS(rs_[:], PS[bsum][:, 0:T], 1.0 / 1024.0, 1e-6, ALU.mult, ALU.add, [("ps", bsum)], [krs])
            if lim < 4.17:
                return
            ACT(rs_[:], rs_[:], AF.Sqrt, [krs], [krs])
            if lim < 4.18:
                return
            RCP(lrs[:], rs_[:], [krs], ["lrs"])

            if lim < 4.2:
                return
            brk = nb(True)
            def rwkv_pair(c):
                sr = nwA()
                WL(wA[sr], [("wA", sr, 0), ("wA", sr, 1)], ("wA", sr),
                   [(wA[sr][:, :, 0:128], winv[:, :, cs(c * 128, (c + 1) * 128)]),
                    (wA[sr][:, :, 128:256], winv[:, :, cs(1088 + c * 128, 1088 + (c + 1) * 128)])])
                yield
                bk = proj(sr, 0, 128)
                yield
                rp, krp = tmpf()
                shift_evac(c, bk, 128, rp[:], krp)
                yield
                bk = proj(sr, 1, 128)
                yield
                kp, kkp = tmpf()
                shift_evac(8 + c, bk, 128, kp[:], kkp)
                yield
                sv = sr
                WL(wA[sv], [("wA", sv, 0), ("wA", sv, 1)], ("wA", sv), [(wA[sv][:, :, 0:128], winv[:, :, cs(2112 + c * 128, 2112 + (c + 1) * 128)])])
                yield
                bk = proj(sv, 0, 128)
                yield
                shift_evac(16 + c, bk, 128, vT[:, c, :], ("vT", c))
                yield
                if lim < 4.21:
                    return
                bk = nb()
                MM(PS[bk][:, 0:T], w2p[:, c * 128:(c + 1) * 128], lorab[:, :], True, True, ["w2p", "lorab"], [("ps", bk)])
                yield
                MM(PS[bk][:, 256:256 + T], a2p[:, c * 128:(c + 1) * 128], lorab[:, :], True, True, ["a2p", "lorab"], [("ps", bk)])
                yield
                S_, kS = tmpf()
                ACT(S_[:], PS[bk][:, 0:T], AF.Tanh, [("ps", bk), "colp"], [kS], scale=0.5, bias=col(HW0 + c))
                yield
                TS(S_[:], S_[:], 1.0, None, ALU.add, None, [kS], [kS])
                yield
                tha, ktha = tmpf()
                ACT(tha[:], PS[bk][:, 256:256 + T], AF.Tanh, [("ps", bk), "colp"], [ktha], scale=0.5, bias=col(HA0 + c))
                yield
                Lc, kLc = tmpf()
                SCAN(Lc[:], rmask[:], S_[:], 0.0, ["rmask", kS], [kLc])
                yield
                eL, keL = tmpf()
                ACT(eL[:], Lc[:], AF.Exp, [kLc], [keL], scale=-CD)
                yield
                TT(act[:, 24 + c, :], rp[:], eL[:], ALU.mult, [krp, keL], [("act", 24 + c)])
                yield
                for n in range(NTB):
                    CP(PC[:, c, n:n + 1], eL[:, n * 128 + 127:n * 128 + 128], [keL], [("PC", n)])
                    yield
                TT(S_[:], Lc[:], S_[:], ALU.subtract, [kLc, kS], [kS])
                yield
                ACT(S_[:], S_[:], AF.Exp, [kS], [kS], scale=-CD)
                yield
                ACT(Lc[:], Lc[:], AF.Exp, [kLc], [kLc], scale=CD)
                yield
                if lim < 4.22:
                    return
                kks, kkks = tmpf()
                TS(kks[:], kp[:], col(KK + c), None, ALU.mult, None, [kkp, "colp"], [kkks])
                yield
                sq, ksq = tmpb()
                ACT(sq[:], kks[:], AF.Square, [kkks], [ksq])
                yield
                bk2 = nb()
                MM(PS[bk2][:, 0:T], bones[:], sq[:], True, True, ["bones", ksq], [("ps", bk2)])
                yield
                nr, knr = tmpf()
                ACT(nr[:], PS[bk2][:, 0:T], AF.Sqrt, [("ps", bk2)], [knr])
                yield
                TS(nr[:], nr[:], 1e-12, None, ALU.max, None, [knr], [knr])
                yield
                rn, krn = tmpf()
                RCP(rn[:], nr[:], [knr], [krn])
                yield
                TT(kks[:], kks[:], rn[:], ALU.mult, [kkks, krn], [kkks])
                yield
                if lim < 4.23:
                    return
                STT(act[:, c, :], kks[:], -1.0, S_[:], ALU.mult, ALU.mult, [kkks, kS], [("act", c)])
                yield
                TS(rn[:], tha[:], 0.5, 0.5, ALU.mult, ALU.add, [ktha], [krn])
                yield
                TT(rn[:], rn[:], kks[:], ALU.mult, [krn, kkks], [krn])
                yield
                TT(act[:, 8 + c, :], rn[:], Lc[:], ALU.mult, [krn, kLc], [("act", 8 + c)])
                yield
                TS(tha[:], tha[:], col(C1 + c), col(C0 + c), ALU.mult, ALU.add, [ktha, "colp"], [ktha])
                yield
                TT(tha[:], tha[:], kp[:], ALU.mult, [ktha, kkp], [ktha])
                yield
                TT(act[:, 16 + c, :], tha[:], Lc[:], ALU.mult, [ktha, kLc], [("act", 16 + c)])
                yield
                if lim < 4.24:
                    return
                rkb, krkb = tmpb()
                STT(rkb[:], rp[:], col(RK + c), tha[:], ALU.mult, ALU.mult, [krp, ktha, "colp"], [krkb])
                yield
                for n in range(NTB):
                    MM(PS[brk][:, n * 16 + 2 * c:n * 16 + 2 * c + 2], rkb[:, n * 128:(n + 1) * 128], hind[:], True, True,
                       [krkb, "hind"], [("ps", brk)])
                    yield
            for cp in range(4):
                run2([rwkv_pair(2 * cp), rwkv_pair(2 * cp + 1)])
            if lim < 4.25:
                cnt["res"].discard(brk)
                return
            CP(rkS[:].rearrange("p a b -> p (a b)"), PS[brk][:, 0:NTB * 16], [("ps", brk)], ["rkS"])
            cnt["res"].discard(brk)
            for c in range(8):
                STT(xnT[:, 8 + c, :], ylb[:, c, :], col(LN + c), lrs[:], ALU.mult, ALU.mult, [("ylb", c), "lrs", "colp"], [("xnT", 8 + c)])

            if lim < 4.3:
                return
            for n in range(NTB):
                ts = slice(n * 128, (n + 1) * 128)
                for (base, src, dst, dk) in ((0, act, tmA, "tmA"), (8, act, tmBk, "tmBk"), (16, act, tmK, "tmK"), (0, vT, tmV, "tmV")):
                    bk = nb()
                    for c in range(8):
                        rk_ = ("vT", c) if src is vT else ("act", base + c)
                        TR(PSB[bk][:, c * 128:(c + 1) * 128], src[:, base + c, ts], [rk_], [("ps", bk)])
                    ACT(dst[:], PSB[bk][:, 0:1024], AF.Copy, [("ps", bk)], [dk])
                def inv_batch(hb4, S):
                    par, half = hb4 // 2, hb4 % 2
                    hs4 = [8 * half + par + 2 * i for i in range(4)]
                    hv = lambda t3: t3[:].rearrange("p (a two) s -> p a two s", two=2)[:, 4 * half:4 * half + 4, par, :]
                    cv = lambda t2: t2[:].rearrange("p (a two d) -> p a two d", two=2, d=64)[:, 4 * half:4 * half + 4, par, :]
                    Pm_, Qm_, Xf_, Xb_ = S["Pm"], S["Qm"], S["Xf"], S["Xb"]
                    kP, kQ, kXf, kXb = S["kP"], S["kQ"], S["kXf"], S["kXb"]

                    def hsl(base, hh):
                        c, r0 = hh // 2, (hh % 2) * 64
                        return act[r0:r0 + 64, base + c, ts], ("act", base + c)

                    def amat(lb, rb, mask, mkey, dst3, dkeys):
                        bk = nb()
                        for i, hh in enumerate(hs4):
                            l_, lk = hsl(lb, hh)
                            r_, rk2 = hsl(rb, hh)
                            MM(PS[bk][:, i * 128:(i + 1) * 128], l_, r_, True, True, [lk, rk2], [("ps", bk)])
                            yield
                        TT(dst3, PS[bk][:, :].rearrange("p (a b) -> p a b", a=4), mask[:], ALU.mult, [("ps", bk), mkey], dkeys)
                        yield

                    yield from amat(8, 0, mU, "mU", Pm_[0], kP[0])
                    yield from amat(0, 8, mL, "mL", Qm_[0], kQ[0])
                    yield from amat(16, 0, mU, "mU", hv(AkT), [("AkT", hb4)])
                    yield from amat(8, 24, mUi, "mUi", hv(ArbT), [("ArbT", hb4)])
                    yield from amat(16, 24, mUi, "mUi", hv(ArkT), [("ArkT", hb4)])
                    bk = nb()
                    for i, hh in enumerate(hs4):
                        MM(PS[bk][:, i * 128 + 64:i * 128 + 128], AkT[:, hh, :], tmV[:, hh * 64:(hh + 1) * 64], True, True,
                           [("AkT", hb4), "tmV"], [("ps", bk)])
                        yield
                    CP(Xf_[:, :, 64:128], PS[bk][:, :].rearrange("p (a b) -> p a b", a=4)[:, :, 64:128], [("ps", bk)], kXf)
                    yield
                    CP(Xf_[:, :, 0:64], cv(tmA), ["tmA"], kXf)
                    yield
                    ACT(Xb_, Xf_, AF.Copy, kXf, kXb)
                    yield
                    cur = 0
                    for lev in range(7):
                        bk = nb()
                        for i in range(4):
                            MM(PS[bk][:, i * 128:(i + 1) * 128], Pm_[cur][:, i, :], Xb_[:, i, :], True, True, kP[cur] + kXb, [("ps", bk)])
                            yield
                        TT(Xf_, Xf_, PS[bk][:, :].rearrange("p (a b) -> p a b", a=4), ALU.add, kXf + [("ps", bk)], kXf)
                        yield
                        ACT(Xb_, Xf_, AF.Copy, kXf, kXb)
                        yield
                        if lev < 6:
                            nxt = 1 - cur
                            bk = nb()
                            for i in range(4):
                                MM(PS[bk][:, i * 128:(i + 1) * 128], Qm_[cur][:, i, :], Pm_[cur][:, i, :], True, True, kP[cur] + kQ[cur], [("ps", bk)])
                                yield
                            bk2 = None
                            if lev < 5:
                                bk2 = nb()
                                for i in range(4):
                                    MM(PS[bk2][:, i * 128:(i + 1) * 128], Pm_[cur][:, i, :], Qm_[cur][:, i, :], True, True, kP[cur] + kQ[cur], [("ps", bk2)])
                                    yield
                            ACT(Pm_[nxt], PS[bk][:, :].rearrange("p (a b) -> p a b", a=4), AF.Copy, [("ps", bk)], kP[nxt])
                            yield
                            if bk2 is not None:
                                CP(Qm_[nxt], PS[bk2][:, :].rearrange("p (a b) -> p a b", a=4), [("ps", bk2)], kQ[nxt])
                                yield
                            cur = nxt
                    bk = nb()
                    r0 = par * 64
                    for i, hh in enumerate(hs4):
                        TR(PSB[bk][r0:r0 + 64, i * 128:(i + 1) * 128], Xb_[:, i, 0:64], kXb, [("ps", bk)])
                        yield
                    ACT(AwT[r0:r0 + 64, 4 * half:4 * half + 4, :], PSB[bk][r0:r0 + 64, 0:512].rearrange("p (a b) -> p a b", a=4), AF.Copy, [("ps", bk)], ["AwT"])
                    yield
                    CP(cv(Uv), Xf_[:, :, 64:128], kXf, ["Uv"])
                    yield

                run2([inv_batch(0, SETS[0]), inv_batch(1, SETS[1])])
                run2([inv_batch(2, SETS[0]), inv_batch(3, SETS[1])])
                bu = [nb(), nb()]
                for hh in range(16):
                    c, r0 = hh // 2, (hh % 2) * 64
                    MM(PS[bu[hh // 8]][:, (hh % 8) * 64:(hh % 8) * 64 + 64], AwT[:, c, :], Hb[:, c, hh % 2, :], True, True,
                       ["AwT", "Hb"], [("ps", bu[hh // 8])])
                for q in range(2):
                    TT(Ub[:, q * 512:(q + 1) * 512], PS[bu[q]][:, :], Uv[:, q * 512:(q + 1) * 512], ALU.add, [("ps", bu[q]), "Uv"], ["Ub"])
                by = [nb(), nb()]
                for hh in range(16):
                    c, r0 = hh // 2, (hh % 2) * 64
                    o = PS[by[hh // 8]][:, (hh % 8) * 64:(hh % 8) * 64 + 64]
                    MM(o, act[:, 24 + c, ts], Hb[:, c, hh % 2, :], True, False, [("act", 24 + c), "Hb"], [("ps", by[hh // 8])])
                    MM(o, ArbT[:, hh, :], Ub[:, hh * 64:(hh + 1) * 64], False, False, [("ArbT", (hh % 2) * 2 + hh // 8), "Ub"], [("ps", by[hh // 8])])
                    MM(o, ArkT[:, hh, :], tmV[:, hh * 64:(hh + 1) * 64], False, True, [("ArkT", (hh % 2) * 2 + hh // 8), "tmV"], [("ps", by[hh // 8])])
                bh = nb()
                for hh in range(16):
                    c, r0 = hh // 2, (hh % 2) * 64
                    o = PS[bh][r0:r0 + 64, c * 64:(c + 1) * 64]
                    MM(o, tmBk[:, hh * 64:(hh + 1) * 64], Ub[:, hh * 64:(hh + 1) * 64], True, False, ["tmBk", "Ub"], [("ps", bh)])
                    MM(o, tmK[:, hh * 64:(hh + 1) * 64], tmV[:, hh * 64:(hh + 1) * 64], False, True, ["tmK", "tmV"], [("ps", bh)])
                TT(Hf[:], Hf[:], PS[bh][:, :].rearrange("p (a b) -> p a b", a=8), ALU.add, ["Hf", ("ps", bh)], ["Hf"])
                TT(Hf[:], Hf[:], PC[:, :, n:n + 1].to_broadcast([128, 8, 64]), ALU.mult, ["Hf", ("PC", n)], ["Hf"])
                ACT(Hb[0:64, :, 0, :], Hf[0:64, :, :], AF.Copy, ["Hf"], ["Hb"])
                ACT(Hb[64:128, :, 1, :], Hf[64:128, :, :], AF.Copy, ["Hf"], ["Hb"])
                for q in range(2):
                    ACT(ybuf[:, q * 512:(q + 1) * 512], PS[by[q]][:, :], AF.Copy, [("ps", by[q])], ["ybuf"])
                y3 = ybuf[:].rearrange("p (a b) -> p a b", a=16)
                t3 = yt1[:].rearrange("p (a b) -> p a b", a=16)
                gs = sb_gn
                RSUM(gs[:, 0:16], y3, ["ybuf"], ["gs"])
                ACT(yt1[:], ybuf[:], AF.Square, ["ybuf"], ["yt1"])
                RSUM(gs[:, 16:32], t3, ["yt1"], ["gs"])
                TS(gs[:, 0:16], gs[:, 0:16], 1.0 / 64.0, None, ALU.mult, None, ["gs"], ["gs"])
                TT(gs[:, 32:48], gs[:, 0:16], gs[:, 0:16], ALU.mult, ["gs"], ["gs"])
                STT(gs[:, 16:32], gs[:, 16:32], 1.0 / 64.0, gs[:, 32:48], ALU.mult, ALU.subtract, ["gs"], ["gs"])
                TS(gs[:, 16:32], gs[:, 16:32], 0.0, 64e-5, ALU.max, ALU.add, ["gs"], ["gs"])
                ACT(gs[:, 16:32], gs[:, 16:32], AF.Sqrt, ["gs"], ["gs"])
                RCP(gs[:, 32:48], gs[:, 16:32], ["gs"], ["gs"])
                TT(y3, y3, gs[:, 0:16].unsqueeze(2).to_broadcast([128, 16, 64]), ALU.subtract, ["ybuf", "gs"], ["ybuf"])
                TT(y3, y3, gs[:, 32:48].unsqueeze(2).to_broadcast([128, 16, 64]), ALU.mult, ["ybuf", "gs"], ["ybuf"])
                TT(ybuf[:], ybuf[:], lnwb[:], ALU.mult, ["ybuf", "lnwb"], ["ybuf"])
                TT(ybuf[:], ybuf[:], lnbb[:], ALU.add, ["ybuf", "lnbb"], ["ybuf"])
                TT(t3, tmV[:].rearrange("p (a b) -> p a b", a=16), rkS[:, n, :].unsqueeze(2).to_broadcast([128, 16, 64]), ALU.mult,
                   ["tmV", "rkS"], ["yt1"])
                TT(ybuf[:], ybuf[:], yt1[:], ALU.add, ["ybuf", "yt1"], ["ybuf"])
                for q in range(2):
                    bk = nb()
                    MM(PS[bk][:, :], sgxa[:, ts], g2a[:, q * 512:(q + 1) * 512], True, False, ["sgxa", "g2a"], [("ps", bk)])
                    MM(PS[bk][:, :], sgxb[:, ts], g2b[:, q * 512:(q + 1) * 512], False, True, ["sgxb", "g2b"], [("ps", bk)])
                    TT(yg[:, q * 512:(q + 1) * 512], ybuf[:, q * 512:(q + 1) * 512], PS[bk][:, :], ALU.mult, ["ybuf", ("ps", bk)], ["yg"])
                bk = nb()
                for c in range(8):
                    TR(PSB[bk][:, c * 128:(c + 1) * 128], yg[:, c * 128:(c + 1) * 128], ["yg"], [("ps", bk)])
                ACT(xnT[:, 0:8, ts], PSB[bk][:, 0:1024].rearrange("p (a b) -> p a b", a=8), AF.Copy, [("ps", bk)], [("xnT", c) for c in range(8)])

            if lim < 4.4:
                return
            for b in range(8):
                s = nwB()
                WL(wB[s], [("wB", s, 0), ("wB", s, 1)], ("wB", s), [(wB[s][:], woutv[:, cs(2 * b, 2 * b + 2), :])])
                for j in range(2):
                    kc = 2 * b + j
                    for tb in range(NTB):
                        for db in range(4):
                            MM(PS[tb * 4 + db][:, :], xnT[:, kc, tb * 128:(tb + 1) * 128], wB[s][:, j, db * 512:(db + 1) * 512],
                               kc == 0, kc == 15, [("xnT", kc), ("wB", s, j)], [("ps", tb * 4 + db)])
            for tb in range(NTB):
                for db in range(4):
                    hs = h[:, tb, db * 512:(db + 1) * 512]
                    TT(hs, hs, PS[tb * 4 + db][:, :], ALU.add, [("ps", tb * 4 + db), ("h", tb)], [("h", tb)])

        sb_gn = sb("gs", [128, 48], F32)
        finals = []
        for ti in range(NT):
            cur["ti"] = ti
            cur["blk"] = 0
            for tb in range(NTB):
                DMA("sp", h[:, tb, :], xv[ti * NTB + tb], [], [("h", tb)], ("hld", tb))
            if lim >= 1:
                norm_stage(G1)
            if lim >= 2:
                ffn_stage(0)
            if lim >= 3:
                norm_stage(GM)
            if lim >= 4:
                mixer_stage(lim)
            if lim >= 5:
                norm_stage(G2)
                ffn_stage(1)
            for tb in range(NTB):
                ss = stat[:, 48:49]
                ACT(xnb[:], h[:, tb, :], AF.Square, [("h", tb)], ["xnb", "stat"], accum=ss)
                rs = rstd_of(ss, 1, 1.0 / D, 1e-6)
                STT(h[:, tb, :], h[:, tb, :], rs, fnb[:], ALU.mult, ALU.mult, [("h", tb), "stat", "fnb"], [("h", tb)])
                finals.append(DMA("sp", ov[ti * NTB + tb], h[:, tb, :], [("h", tb)], [], ("hst", tb)))
        P.emit(st, finals + st_tokens)
    return nc


def _pack(inputs):
    f = lambda k: np.asarray(inputs[k], dtype=np.float32)
    sq = lambda k: f(k)[0]
    colp = np.zeros((128, NCOL_IN), np.float32)
    pc = lambda v: np.ascontiguousarray(v.reshape(-1, 128).T)
    colp[:, G1:G1 + 16] = pc(sq("ffn1_norm"))
    colp[:, GM:GM + 16] = pc(sq("mix_norm"))
    colp[:, G2:G2 + 16] = pc(sq("ffn2_norm"))
    mu = sq("rwkv_mu")
    colp[:, MU:MU + 8] = pc(mu[0:1024])
    colp[:, MU + 8:MU + 16] = pc(mu[1088:2112])
    colp[:, MU + 16:MU + 24] = pc(mu[2112:3136])
    colp[0:64, MU + 24] = mu[1024:1088]
    colp[64:128, MU + 24] = mu[3136:3200]
    colp[:, MU + 25] = mu[3200:3328]
    colp[0:32, MU + 26] = mu[3328:3360]
    colp[:, W0:W0 + 8] = pc(sq("rwkv_w0"))
    colp[:, A0:A0 + 8] = pc(sq("rwkv_a0"))
    colp[:, KK:KK + 8] = pc(sq("rwkv_k_k"))
    colp[:, KA:KA + 8] = pc(sq("rwkv_k_a"))
    colp[:, RK:RK + 8] = pc(sq("rwkv_r_k").reshape(-1))
    cw = sq("lru_conv_w")
    for j in range(4):
        colp[:, CW + j * 8:CW + j * 8 + 8] = pc(cw[j])
    colp[:, CB:CB + 8] = pc(sq("lru_conv_b"))
    colp[:, BA:BA + 8] = pc(sq("lru_ba"))
    colp[:, BX:BX + 8] = pc(sq("lru_bx"))
    colp[:, LAM:LAM + 8] = pc(sq("lru_lam"))
    colp[:, LN:LN + 8] = pc(sq("lru_norm"))
    w2p = np.zeros((128, 1024), np.float32)
    w2p[0:64] = sq("rwkv_w2")
    a2p = np.zeros((128, 1024), np.float32)
    a2p[64:128] = sq("rwkv_a2")

    def bd(w):
        o = np.zeros((128, 8, 128), np.float32)
        for c in range(8):
            o[0:64, c, 0:64] = w[2 * c]
            o[64:128, c, 64:128] = w[2 * c + 1]
        return o

    shared = dict(
        wg1=sq("ffn1_w_gate"), wu1=sq("ffn1_w_up"), wd1=sq("ffn1_w_down"),
        wg2=sq("ffn2_w_gate"), wu2=sq("ffn2_w_up"), wd2=sq("ffn2_w_down"),
        w_in=sq("w_in"), w_out=sq("w_out"), colp=colp, w2p=w2p, a2p=a2p,
        g2=sq("rwkv_g2"), wabd=bd(sq("lru_wa")), wxbd=bd(sq("lru_wx")),
        lnwb=np.ascontiguousarray(np.broadcast_to(sq("rwkv_ln_w")[None, :], (128, 1024))),
        lnbb=np.ascontiguousarray(np.broadcast_to(sq("rwkv_ln_b")[None, :], (128, 1024))),
        fnb=np.ascontiguousarray(np.broadcast_to(f("final_norm")[None, :], (128, D))),
    )
    return shared


_NT = SEQ // T


def kernel(**inputs):
    x = np.asarray(inputs["x"], dtype=np.float32)
    shared = _pack(inputs)
    nc = build(_NT)
    in_maps = [dict(shared, x=np.ascontiguousarray(x[b])) for b in range(N_CORES)]
    res = run_bass_kernel_spmd(nc, in_maps, core_ids=list(range(N_CORES)))
    return np.stack([res.results[b]["out"] for b in range(N_CORES)], axis=0)
```
